# Optimizing a Trainium2 kernel written in Bass

```python
import jax, jax.numpy as jnp
from jax import lax
import numpy as np


D_MODEL = 1024
BATCH = 8
SEQ = 2048
DEPTH = 4
DEC_BATCH = 128
DEC_SEQ = 4
PAST_LEN = 16384
PAGE_SIZE = 128

N_MEM = 256
BRANCH_W = D_MODEL // 2
CONV_A_W = 3
CHUNK = 128
GMLP_GROUPS = 4
GMLP_GROUP_W = BRANCH_W // GMLP_GROUPS
CONV_C_W = 31
X_HEADS = 4
X_HEAD_DIM = BRANCH_W // X_HEADS
D_FF = 4 * D_MODEL
N_BRANCH = 4
IN_COLS = 8 * BRANCH_W + N_BRANCH * D_MODEL
ALPHA = (2 * DEPTH) ** 0.25
BETA = (8 * DEPTH) ** -0.25
LN_EPS = 1e-5

kernel_name = "hybrid_gated_conv_gmlp_conformer_memxattn_step"


def layer_norm(x, g, b):
    xf = x.astype(jnp.float32)
    mu = jnp.mean(xf, axis=-1, keepdims=True)
    var = jnp.mean(jnp.square(xf - mu), axis=-1, keepdims=True)
    return ((xf - mu) * lax.rsqrt(var + LN_EPS) * g.astype(jnp.float32) + b.astype(jnp.float32)).astype(x.dtype)


def causal_dwconv(x, buf, w):
    L, C = x.shape[1], x.shape[2]
    xp = jnp.concatenate([buf.astype(x.dtype), x], axis=1)
    y = lax.conv_general_dilated(xp, w[:, None, :].astype(x.dtype), window_strides=(1,), padding='VALID',
                                 dimension_numbers=('NWC', 'WIO', 'NWC'), feature_group_count=C)
    return y, xp[:, L:]


def spatial_gate(v, w_s, b_s):
    Bn, L, _ = v.shape
    T = min(L, CHUNK)
    n = L // T
    vc = v.reshape(Bn, n, T, GMLP_GROUPS, GMLP_GROUP_W)
    mask = jnp.tril(jnp.ones((T, T), dtype=bool))
    ws = jnp.where(mask[None], w_s[:, :T, :T], 0).astype(v.dtype)
    s = jnp.einsum('gts,bnsgc->bntgc', ws, vc) + b_s[:, :T].T[None, None, :, :, None].astype(v.dtype)
    return s.reshape(Bn, L, BRANCH_W)


def mem_attention(q, k, v):
    s = jnp.einsum('blhd,bmhd->bhlm', q, k).astype(jnp.float32) * (X_HEAD_DIM ** -0.5)
    p = jax.nn.softmax(s, axis=-1).astype(v.dtype)
    return jnp.einsum('bhlm,bmhd->blhd', p, v)


def trunk_layer(x, buf_a, buf_c, mem_k, mem_v, w_in, w_conv_a, ln_v_g, ln_v_b, w_s, b_s,
                w_conv_c, b_conv_c, ln_c_g, ln_c_b, w_out_br, w_o, ln1_g, ln1_b,
                w_up, b_up, w_down, b_down, ln2_g, ln2_b):
    Bn, L, _ = x.shape
    W = BRANCH_W
    z = x @ w_in
    xa, ga, gc, zb, zc, q, zg = jnp.split(z, [W, 2 * W, 3 * W, 5 * W, 7 * W, 8 * W], axis=-1)
    conv_a, new_buf_a = causal_dwconv(gc * xa, buf_a, w_conv_a)
    y_a = ga * conv_a
    zb = jax.nn.gelu(zb)
    u, v = zb[..., :W], layer_norm(zb[..., W:], ln_v_g, ln_v_b)
    y_b = u * spatial_gate(v, w_s, b_s)
    glu = zc[..., :W] * jax.nn.sigmoid(zc[..., W:])
    conv_c, new_buf_c = causal_dwconv(glu, buf_c, w_conv_c)
    y_c = jax.nn.silu(layer_norm(conv_c + b_conv_c, ln_c_g, ln_c_b))
    y_x = mem_attention(q.reshape(Bn, L, X_HEADS, X_HEAD_DIM), mem_k, mem_v).reshape(Bn, L, W)
    ys = jnp.stack([y_a, y_b, y_c, y_x], axis=2)
    branch = jnp.einsum('blnw,nwd->blnd', ys, w_out_br)
    gates = jax.nn.sigmoid(zg.reshape(Bn, L, N_BRANCH, D_MODEL))
    mix = jnp.sum(gates * branch, axis=2) @ w_o
    x = layer_norm(ALPHA * x + mix, ln1_g, ln1_b)
    h = jnp.square(jax.nn.relu(x @ w_up + b_up)) @ w_down + b_down
    x = layer_norm(ALPHA * x + h, ln2_g, ln2_b)
    return x, new_buf_a, new_buf_c, v


def setup_inputs(seed: int = 0) -> dict:
    key = jax.random.key(seed)
    ks = jax.random.split(key, 32)
    nrm = lambda k, shape, s: jax.random.normal(k, shape, jnp.float32) * s
    W = BRANCH_W
    return {
        "x_prompt": nrm(ks[0], (BATCH, SEQ, D_MODEL), 1.0),
        "x_sample": nrm(ks[1], (DEC_BATCH, DEC_SEQ, D_MODEL), 1.0),
        "mem_prompt": nrm(ks[2], (BATCH, N_MEM, D_MODEL), 1.0),
        "state_conv_a": nrm(ks[3], (DEPTH, DEC_BATCH, CONV_A_W - 1, W), 1.0),
        "state_conv_c": nrm(ks[4], (DEPTH, DEC_BATCH, CONV_C_W - 1, W), 1.0),
        "cache_mem_k": nrm(ks[5], (DEPTH, DEC_BATCH, N_MEM, X_HEADS, X_HEAD_DIM), 1.0),
        "cache_mem_v": nrm(ks[6], (DEPTH, DEC_BATCH, N_MEM, X_HEADS, X_HEAD_DIM), 1.0),
        "w_in": nrm(ks[7], (DEPTH, D_MODEL, IN_COLS), D_MODEL ** -0.5),
        "w_conv_a": nrm(ks[8], (DEPTH, CONV_A_W, W), CONV_A_W ** -0.5),
        "ln_v_g": 1.0 + nrm(ks[9], (DEPTH, W), 0.02),
        "ln_v_b": nrm(ks[10], (DEPTH, W), 0.02),
        "w_s": nrm(ks[11], (DEPTH, GMLP_GROUPS, CHUNK, CHUNK), 0.5 * CHUNK ** -0.5),
        "b_s": 1.0 + nrm(ks[12], (DEPTH, GMLP_GROUPS, CHUNK), 0.02),
        "w_conv_c": nrm(ks[13], (DEPTH, CONV_C_W, W), CONV_C_W ** -0.5),
        "b_conv_c": nrm(ks[14], (DEPTH, W), 0.02),
        "ln_c_g": 1.0 + nrm(ks[15], (DEPTH, W), 0.02),
        "ln_c_b": nrm(ks[16], (DEPTH, W), 0.02),
        "w_mem_kv": nrm(ks[17], (DEPTH, D_MODEL, 2 * W), D_MODEL ** -0.5),
        "w_out_br": nrm(ks[18], (DEPTH, N_BRANCH, W, D_MODEL), BETA * W ** -0.5),
        "w_o": nrm(ks[19], (DEPTH, D_MODEL, D_MODEL), BETA * D_MODEL ** -0.5),
        "ln1_g": 1.0 + nrm(ks[20], (DEPTH, D_MODEL), 0.02),
        "ln1_b": nrm(ks[21], (DEPTH, D_MODEL), 0.02),
        "w_up": nrm(ks[22], (DEPTH, D_MODEL, D_FF), D_MODEL ** -0.5),
        "b_up": nrm(ks[23], (DEPTH, D_FF), 0.02),
        "w_down": nrm(ks[24], (DEPTH, D_FF, D_MODEL), BETA * D_FF ** -0.5),
        "b_down": nrm(ks[25], (DEPTH, D_MODEL), 0.02),
        "ln2_g": 1.0 + nrm(ks[26], (DEPTH, D_MODEL), 0.02),
        "ln2_b": nrm(ks[27], (DEPTH, D_MODEL), 0.02),
    }


def reference(x_prompt, x_sample, mem_prompt, state_conv_a, state_conv_c, cache_mem_k, cache_mem_v,
              w_in, w_conv_a, ln_v_g, ln_v_b, w_s, b_s, w_conv_c, b_conv_c, ln_c_g, ln_c_b,
              w_mem_kv, w_out_br, w_o, ln1_g, ln1_b, w_up, b_up, w_down, b_down, ln2_g, ln2_b):
    W = BRANCH_W
    Bp = x_prompt.shape[0]
    n_mem = mem_prompt.shape[1]
    zero_a = jnp.zeros((Bp, CONV_A_W - 1, W), x_prompt.dtype)
    zero_c = jnp.zeros((Bp, CONV_C_W - 1, W), x_prompt.dtype)
    yp, ys = x_prompt, x_sample
    pa, pc, pk, pv, sa, sc, sv = [], [], [], [], [], [], []
    for l in range(DEPTH):
        lw = (w_in[l], w_conv_a[l], ln_v_g[l], ln_v_b[l], w_s[l], b_s[l], w_conv_c[l], b_conv_c[l],
              ln_c_g[l], ln_c_b[l], w_out_br[l], w_o[l], ln1_g[l], ln1_b[l], w_up[l], b_up[l],
              w_down[l], b_down[l], ln2_g[l], ln2_b[l])
        kv = jnp.einsum('bmd,de->bme', mem_prompt, w_mem_kv[l])
        mk = kv[..., :W].reshape(Bp, n_mem, X_HEADS, X_HEAD_DIM)
        mv = kv[..., W:].reshape(Bp, n_mem, X_HEADS, X_HEAD_DIM)
        yp, nba, nbc, _ = trunk_layer(yp, zero_a, zero_c, mk, mv, *lw)
        pa.append(nba); pc.append(nbc); pk.append(mk); pv.append(mv)
        ys, sba, sbc, v_new = trunk_layer(ys, state_conv_a[l], state_conv_c[l], cache_mem_k[l], cache_mem_v[l], *lw)
        sa.append(sba); sc.append(sbc); sv.append(v_new)
    return (yp, ys, jnp.stack(pa), jnp.stack(pc), jnp.stack(pk), jnp.stack(pv),
            jnp.stack(sa), jnp.stack(sc), jnp.stack(sv))
```

```python
import numpy as np
import concourse.bass as bass
import concourse.mybir as mybir
from concourse.bass_utils import run_bass_kernel_spmd

F32 = mybir.dt.float32
BF16 = mybir.dt.bfloat16
AF = mybir.ActivationFunctionType
ALU = mybir.AluOpType

D = 1024
W = 512
NMEM = 256
DFF = 4096
INC = 8192
ALPHA = float((2 * 4) ** 0.25)
EPS = 1e-5
QSCALE = float(128 ** -0.5)

WC, BC, WA, LCG, LCB, L1G, L1B, BUP, BDN, L2G, L2B, AG1, AB1, AG2, AB2 = (
    0, 124, 128, 140, 144, 148, 156, 164, 196, 204, 212, 220, 228, 236, 244)


import os


class StopBuild(Exception):
    pass


class DSem:
    def __init__(self, nc, name):
        self.sem = nc.alloc_semaphore(name)
        self.cnt = 0


class Builder:
    def __init__(self, SEQ, NB, DEPTH, NSLOT=4):
        self.SEQ, self.NB, self.DEPTH = SEQ, NB, DEPTH
        self.PH = SEQ // 2
        self.NBH = NB // 2
        self.S = self.NBH * 4
        self.T = self.PH + self.S
        self.NM = self.PH // 128
        self.tts = [(s, min(512, self.PH - s)) for s in range(0, self.PH, 512)] + [(self.PH, self.S)]
        self.NSLOT = NSLOT
        self.nc = bass.Bass("TRN2", target_bir_lowering=False)
        nc = self.nc
        self.E = {'pe': nc.tensor, 'act': nc.scalar, 'dve': nc.vector, 'pool': nc.gpsimd, 'sp': nc.sync}
        self.sem = {}
        self.cnt = {}
        self.waited = {e: {} for e in self.E}
        self.lw = {}
        self.rd = {}
        self.phase_id = 0
        self.store_sems = []
        self.arena_load_sems = []
        self.bank_i = 0
        self.pinned = set()

    def new_phase(self):
        for e in ('pe', 'act', 'dve', 'pool'):
            self.sem[e] = self.nc.alloc_semaphore(f"s_{e}_{self.phase_id}")
            self.cnt[e] = 0
        self.phase_id += 1

    def _waits(self, e, R, Wr):
        toks = {}
        for k in R:
            t = self.lw.get(k)
            if t is not None:
                toks[(t[0].num, t[1])] = t
        for k in Wr:
            t = self.lw.get(k)
            if t is not None:
                toks[(t[0].num, t[1])] = t
            for t in self.rd.get(k, {}).values():
                toks[(t[0].num, t[1])] = t
        eng = self.E[e]
        wd = self.waited[e]
        for t in toks.values():
            sem, val, src = t
            if src == 'pe' and e == 'pe':
                continue
            if wd.get(sem.num, 0) >= val:
                continue
            eng.wait_ge(sem, val)
            wd[sem.num] = val

    def _record(self, tok, R, Wr):
        for k in Wr:
            self.lw[k] = tok
            self.rd[k] = {}
        for k in R:
            d = self.rd.setdefault(k, {})
            old = d.get(tok[0].num)
            if old is None or old[1] < tok[1]:
                d[tok[0].num] = tok

    def op(self, e, fn, R=(), Wr=()):
        psr = [k for k in R if isinstance(k, tuple) and k[0] == 'ps']
        if psr:
            R = [k for k in R if not (isinstance(k, tuple) and k[0] == 'ps')]
            Wr = list(Wr) + psr
        self._waits(e, R, Wr)
        ins = fn()
        self.cnt[e] += 1
        ins.then_inc(self.sem[e], 1)
        tok = (self.sem[e], self.cnt[e], e)
        self._record(tok, R, Wr)
        return tok

    def dma(self, e, out, in_, dsem, R=(), Wr=(), group=None):
        self._waits(e, R, Wr)
        ins = self.E[e].dma_start(out=out, in_=in_)
        dsem.cnt += 16
        ins.then_inc(dsem.sem, 16)
        if group is not None:
            group.append((dsem, R, Wr))
            return None
        tok = (dsem.sem, dsem.cnt, 'dma')
        self._record(tok, R, Wr)
        return tok

    def commit(self, group):
        for (dsem, R, Wr) in group:
            self._record((dsem.sem, dsem.cnt, 'dma'), R, Wr)
        del group[:]

    def barrier(self):
        items = [(self.sem[f], self.cnt[f], f) for f in ('pe', 'act', 'dve', 'pool') if self.cnt[f] > 0]
        items += [(ds.sem, ds.cnt, 'dma') for ds in self.store_sems + self.arena_load_sems if ds.cnt > 0]
        for e in ('pe', 'act', 'dve', 'pool', 'sp'):
            for (sem, val, f) in items:
                if f == e and e == 'pe':
                    continue
                if self.waited[e].get(sem.num, 0) >= val:
                    continue
                self.E[e].wait_ge(sem, val)
                self.waited[e][sem.num] = val

    def snapshot(self):
        items = [(self.sem[f], self.cnt[f], f) for f in ('pe', 'act', 'dve', 'pool') if self.cnt[f] > 0]
        items += [(ds.sem, ds.cnt, 'dma') for ds in self.store_sems + self.arena_load_sems if ds.cnt > 0]
        return items

    def apply_snapshot(self, items):
        for e in ('pe', 'act', 'dve', 'pool', 'sp'):
            for (sem, val, f) in items:
                if self.waited[e].get(sem.num, 0) >= val:
                    continue
                self.E[e].wait_ge(sem, val)
                self.waited[e][sem.num] = val

    def bank(self, pin=False):
        i = self.bank_i
        while i in self.pinned:
            i = (i + 1) % 8
        self.bank_i = (i + 1) % 8
        if pin:
            self.pinned.add(i)
        return self.ps[i], ('ps', i)

    def unpin(self, key):
        self.pinned.discard(key[1])

    def mm(self, out, pairs, R, Wr):
        nc = self.nc

        def fn():
            n = len(pairs)
            ins = None
            for i, (l, r) in enumerate(pairs):
                ins = nc.tensor.matmul(out, l, r, start=(i == 0), stop=(i == n - 1))
            return ins
        return self.op('pe', fn, R, Wr)

    def build(self):
        try:
            self._build()
        except StopBuild:
            pass
        nc = self.nc
        for ds in self.store_sems:
            if ds.cnt > 0:
                nc.sync.wait_ge(ds.sem, ds.cnt)
        return nc

    def ck(self, stage):
        if int(os.environ.get('KSTOP', '99')) == stage:
            raise StopBuild()

    def _build(self):
        nc = self.nc
        SEQ, NB, DEPTH, PH, NBH, S, T, NM = self.SEQ, self.NB, self.DEPTH, self.PH, self.NBH, self.S, self.T, self.NM
        op, dma, mm, bank = self.op, self.dma, self.mm, self.bank
        tts = self.tts
        NT = 2 + 2 * NBH + 30 + 4 * NBH
        TA_P, TA_S, TC_P, TC_S = 0, 2, 2 + 2 * NBH, 32 + 2 * NBH
        self.new_phase()

        def din(name, shape):
            return nc.dram_tensor(name, list(shape), F32, kind="ExternalInput").ap()

        def dout(name, shape):
            return nc.dram_tensor(name, list(shape), F32, kind="ExternalOutput").ap()

        x_prompt = din("x_prompt", [SEQ, D])
        x_sample = din("x_sample", [NB * 4, D])
        mem_prompt = din("mem_prompt", [NMEM, D])
        state_a = din("state_conv_a", [DEPTH, NB * 2, W])
        state_c = din("state_conv_c", [DEPTH, NB, 30, W])
        cache_k = din("cache_mem_k", [DEPTH, NB, NMEM, W])
        cache_v = din("cache_mem_v", [DEPTH, NB, NMEM, W])
        w_in = din("w_in", [DEPTH, D, INC])
        w_conv_a = din("w_conv_a", [DEPTH, 12, 128])
        ln_v_g = din("ln_v_g", [DEPTH, W])
        ln_v_b = din("ln_v_b", [DEPTH, W])
        w_s = din("w_s", [DEPTH, 4, 128, 128])
        b_s = din("b_s", [DEPTH, 4, 128])
        w_conv_c = din("w_conv_c", [DEPTH, 124, 128])
        b_conv_c = din("b_conv_c", [DEPTH, 4, 128])
        ln_c_g = din("ln_c_g", [DEPTH, 4, 128])
        ln_c_b = din("ln_c_b", [DEPTH, 4, 128])
        w_mem_kv = din("w_mem_kv", [DEPTH, D, 2 * W])
        w_out_br = din("w_out_br", [DEPTH, 4, W, D])
        w_o = din("w_o", [DEPTH, D, D])
        ln1_g = din("ln1_g", [DEPTH, 8, 128])
        ln1_b = din("ln1_b", [DEPTH, 8, 128])
        w_up = din("w_up", [DEPTH, D, DFF])
        b_up = din("b_up", [DEPTH, 32, 128])
        w_down = din("w_down", [DEPTH, DFF, D])
        b_down = din("b_down", [DEPTH, 8, 128])
        ln2_g = din("ln2_g", [DEPTH, 8, 128])
        ln2_b = din("ln2_b", [DEPTH, 8, 128])

        y_prompt = dout("y_prompt", [SEQ, D])
        y_sample = dout("y_sample", [NB * 4, D])
        o_ca_p = dout("new_conv_a_prompt", [DEPTH, 2, W])
        o_cc_p = dout("new_conv_c_prompt", [DEPTH, 30, W])
        o_mk = dout("mem_k_prompt", [DEPTH, NMEM, W])
        o_mv = dout("mem_v_prompt", [DEPTH, NMEM, W])
        o_ca_s = dout("new_conv_a_sample", [DEPTH, NB * 2, W])
        o_cc_s = dout("new_conv_c_sample", [DEPTH, NB, 30, W])
        o_v_s = dout("chunk_v_sample", [DEPTH, NB * 4, W])

        def sb(name, shape, dt):
            return nc.alloc_sbuf_tensor(name, list(shape), dt)

        ident_f = sb("ident_f", [128, 128], F32)
        ident_b = sb("ident_b", [128, 128], BF16)
        onesW = sb("onesW", [128, 128], F32)
        onesD = sb("onesD", [128, 128], F32)
        ones_b = sb("ones_b", [128, 128], BF16)
        onesWb = sb("onesWb", [128, 128], BF16)
        onesDb = sb("onesDb", [128, 128], BF16)
        ones_row = sb("ones_row", [1, 128], F32)
        neghalf = sb("neghalf", [128, 512], F32)
        pT = sb("pT", [128, DEPTH, 256], F32)
        memT = sb("memT", [128, 8, NMEM], BF16)
        Rr = sb("R", [128, 8, T], F32)
        xbf = sb("xbf", [128, 8, T], BF16)
        wsT = sb("wsT", [128, 4, 128], BF16)
        wsTf = sb("wsTf", [128, 4, 128], F32)
        wblk = sb("wblk", [32, 4, 32], BF16)
        wblkf = sb("wblkf", [32, 4, 32], F32)
        bs_row = sb("bs_row", [1, 4, 128], F32)
        bsS_row = sb("bsS_row", [1, 4, NBH, 4], F32)
        lnv_g = sb("lnv_g", [128, W], F32)
        lnv_b = sb("lnv_b", [128, W], F32)
        pa_hist = sb("pa_hist", [128, DEPTH, 4, 2], F32)
        glu_hist = sb("glu_hist", [128, DEPTH, 4, 30], F32)
        p_s = sb("p_s", [128, 4, NBH, 6], F32)
        glu_s = sb("glu_s", [128, 4, NBH, 34], F32)
        tails = sb("tails", [128, 4, NT], F32)
        tailsT = sb("tailsT", [128, 512], F32)
        st_stage = sb("st_stage", [128, 2, 512], F32)
        ring = [sb(f"ring{i}", [128, 4096], BF16) for i in range(self.NSLOT)]
        ring_sem = [DSem(nc, f"ringsem{i}") for i in range(self.NSLOT)]
        self.ps = [nc.alloc_psum_tensor(f"ps{i}", [128, 512], F32) for i in range(8)]

        rem = nc.sbuf_bytes_remaining
        ARENA_B = (rem - 1024) // 64 * 64
        arena = sb("arena", [128, ARENA_B // 4], F32)

        class Carver:
            def __init__(s):
                s.off = 0

            def reset(s, off=0):
                s.off = off

            def get(s, shape, dt):
                esz = 4 if dt == F32 else 2
                n = int(np.prod(shape[1:]))
                nbytes = (n * esz + 63) // 64 * 64
                assert s.off + nbytes <= ARENA_B, (s.off, nbytes, ARENA_B)
                v = arena[0:shape[0], s.off // 4:(s.off + nbytes) // 4]
                if dt != F32:
                    v = v.bitcast(dt)
                v = v[:, 0:n]
                if len(shape) == 3:
                    v = v.rearrange("p (a b) -> p a b", a=shape[1])
                elif len(shape) == 4:
                    v = v.rearrange("p (a b c) -> p a b c", a=shape[1], b=shape[2])
                s.off += nbytes
                return v

        cv = Carver()
        ya = cv.get([128, 4, T], BF16)
        yb = cv.get([128, 4, T], BF16)
        yc = cv.get([128, 4, T], BF16)
        yx = cv.get([128, 4, T], BF16)
        ys = [ya, yb, yc, yx]
        SCR0 = cv.off
        mixin = cv.get([128, 8, T], BF16)
        SCRG = cv.off

        dsem_misc = DSem(nc, "d_misc")
        dsem_x = [DSem(nc, "d_x0"), DSem(nc, "d_x1")]
        dsem_y = [DSem(nc, "d_y0"), DSem(nc, "d_y1")]
        dsem_tail = DSem(nc, "d_tail")
        dsem_kvo = [DSem(nc, "d_kvo0"), DSem(nc, "d_kvo1")]
        dsem_vo = DSem(nc, "d_vo")
        dsem_st = DSem(nc, "d_st")
        dsem_lay = DSem(nc, "d_lay")
        dsem_ks = [DSem(nc, f"d_ks{i}") for i in range(4)]
        dsem_vs = [DSem(nc, "d_vs0"), DSem(nc, "d_vs1")]
        dsem_d2d = DSem(nc, "d_d2d")
        self.store_sems = [dsem_y[0], dsem_y[1], dsem_tail, dsem_kvo[0], dsem_kvo[1], dsem_vo, dsem_d2d]
        self.arena_load_sems = [dsem_misc, dsem_x[0], dsem_x[1]] + dsem_ks + dsem_vs

        op('pool', lambda: nc.gpsimd.memset(ident_f[:], 0.0), Wr=['ident_f'])
        op('pool', lambda: nc.gpsimd.affine_select(out=ident_f[:], in_=ident_f[:], compare_op=ALU.not_equal,
                                                   fill=1.0, base=0, pattern=[[-1, 128]], channel_multiplier=1),
           R=['ident_f'], Wr=['ident_f'])
        op('dve', lambda: nc.vector.tensor_copy(out=ident_b[:], in_=ident_f[:]), R=['ident_f'], Wr=['ident_b'])
        op('dve', lambda: nc.vector.memset(onesW[:], 1.0 / W), Wr=['onesW'])
        op('dve', lambda: nc.vector.memset(onesD[:], 1.0 / D), Wr=['onesD'])
        op('dve', lambda: nc.vector.memset(ones_b[:], 1.0), Wr=['ones_b'])
        op('dve', lambda: nc.vector.memset(onesWb[:], 1.0 / W), Wr=['onesWb'])
        op('dve', lambda: nc.vector.memset(onesDb[:], 1.0 / D), Wr=['onesDb'])
        op('dve', lambda: nc.vector.memset(ones_row[:], 1.0), Wr=['ones_row'])
        op('dve', lambda: nc.vector.memset(neghalf[:], -0.5), Wr=['neghalf'])
        op('dve', lambda: nc.vector.memset(tails[:], 0.0), Wr=['tails'])
        op('dve', lambda: nc.vector.memset(wblkf[:], 0.0), Wr=['wblkf'])

        cv.reset(SCR0)
        pstage = cv.get([128, DEPTH, 2, 128], F32)
        mem_sb = cv.get([128, 2, D], F32)
        op('dve', lambda: nc.vector.memset(pstage[:], 0.0), Wr=['pstage'])
        pskeys = {}
        grp = []
        for l in range(DEPTH):
            rows0 = [(w_conv_c[l], 0, 124), (b_conv_c[l], 124, 4)]
            rows1 = [(w_conv_a[l], 0, 12), (ln_c_g[l], 12, 4), (ln_c_b[l], 16, 4), (ln1_g[l], 20, 8),
                     (ln1_b[l], 28, 8), (b_up[l], 36, 32), (b_down[l], 68, 8), (ln2_g[l], 76, 8), (ln2_b[l], 84, 8)]
            for slot, rows in ((0, rows0), (1, rows1)):
                for (src, r0, n) in rows:
                    dma('sp', pstage[r0:r0 + n, l, slot, :], src, dsem_misc, R=['pstage'], Wr=[('pstage', l, slot, r0)], group=grp)
                    pskeys.setdefault((l, slot), []).append(('pstage', l, slot, r0))
        dma('sp', mem_sb[:], mem_prompt.rearrange("(mc p) d -> p mc d", p=128), dsem_misc, Wr=['mem_sb'], group=grp)
        self.commit(grp)
        for l in range(DEPTH):
            pb, pk = bank()
            for slot in range(2):
                op('pe', lambda slot=slot: nc.tensor.transpose(out=pb[:, slot * 128:(slot + 1) * 128],
                                                               in_=pstage[:, l, slot, :], identity=ident_f[:]),
                   R=['pstage', 'ident_f'] + pskeys[(l, slot)], Wr=[pk])
            op('dve', lambda: nc.vector.tensor_copy(out=pT[:, l, :], in_=pb[:, 0:256]), R=[pk], Wr=[('pT', l)])
            op('dve', lambda: nc.vector.tensor_scalar(out=pT[:, l, AG1:AG1 + 8], in0=pT[:, l, L1G:L1G + 8],
                                                      scalar1=ALPHA, scalar2=None, op0=ALU.mult),
               R=[('pT', l)], Wr=[('pT', l)])
            op('dve', lambda: nc.vector.scalar_tensor_tensor(out=pT[:, l, AB1:AB1 + 8], in0=pT[:, l, L1B:L1B + 8],
                                                             scalar=ALPHA, in1=pT[:, l, BDN:BDN + 8],
                                                             op0=ALU.mult, op1=ALU.add),
               R=[('pT', l)], Wr=[('pT', l)])
            op('dve', lambda: nc.vector.tensor_scalar(out=pT[:, l, AG2:AG2 + 8], in0=pT[:, l, L2G:L2G + 8],
                                                      scalar1=ALPHA, scalar2=None, op0=ALU.mult),
               R=[('pT', l)], Wr=[('pT', l)])
            op('dve', lambda: nc.vector.tensor_scalar(out=pT[:, l, AB2:AB2 + 8], in0=pT[:, l, L2B:L2B + 8],
                                                      scalar1=ALPHA, scalar2=None, op0=ALU.mult),
               R=[('pT', l)], Wr=[('pT', l)])

        def pcol(l, c):
            return pT[:, l, c:c + 1]

        for mc in range(2):
            for kg in range(2):
                pb, pk = bank()
                for i in range(4):
                    kc = kg * 4 + i
                    op('pe', lambda i=i, kc=kc: nc.tensor.transpose(out=pb[:, i * 128:(i + 1) * 128],
                                                                    in_=mem_sb[:, mc, kc * 128:(kc + 1) * 128],
                                                                    identity=ident_f[:]),
                       R=['mem_sb', 'ident_f'], Wr=[pk])
                op('act', lambda: nc.scalar.copy(out=memT[:, kg * 4:kg * 4 + 4, mc * 128:(mc + 1) * 128],
                                                 in_=pb[:, :].rearrange("p (a b) -> p a b", a=4)),
                   R=[pk], Wr=['memT'])

        def wplan(l):
            P = []
            wi = w_in[l]
            for j in range(4):
                src = wi[:, 0:1536].rearrange("(kc p) (s q c) -> p kc s q c", p=128, s=3, q=4)[:, :, :, j, :]
                P.append((f"A{j}", [128, 8, 3, 128], src))
            P.append(("U", [128, 8, 512], wi[:, 1536:2048].rearrange("(kc p) n -> p kc n", p=128)))
            P.append(("V", [128, 8, 512], wi[:, 2048:2560].rearrange("(kc p) n -> p kc n", p=128)))
            for jt in range(2):
                src = wi[:, 2560:3584].rearrange("(kc p) (s q c) -> p kc s q c", p=128, s=2, q=2)[:, :, :, jt, :]
                P.append((f"C{jt}", [128, 8, 2, 256], src))
            P.append(("KVK", [128, 8, 512], w_mem_kv[l][:, 0:512].rearrange("(kc p) n -> p kc n", p=128)))
            P.append(("KVV", [128, 8, 512], w_mem_kv[l][:, 512:1024].rearrange("(kc p) n -> p kc n", p=128)))
            P.append(("Q", [128, 8, 512], wi[:, 3584:4096].rearrange("(kc p) n -> p kc n", p=128)))
            for dc in range(8):
                src = wi[:, 4096:8192].rearrange("(kc p) (n q c) -> p kc n q c", p=128, n=4, q=8)[:, :, :, dc, :]
                P.append((f"G{dc}", [128, 8, 4, 128], src))
                src = w_out_br[l].rearrange("n (wc p) (q c) -> p n wc q c", p=128, q=8)[:, :, :, dc, :]
                P.append((f"BR{dc}", [128, 4, 4, 128], src))
            for hf in range(2):
                P.append((f"WO{hf}", [128, 8, 512], w_o[l][:, hf * 512:(hf + 1) * 512].rearrange("(kc p) n -> p kc n", p=128)))
            for fh in range(2):
                for g in range(4):
                    gg = fh * 4 + g
                    P.append((f"UP{gg}", [128, 8, 512], w_up[l][:, gg * 512:(gg + 1) * 512].rearrange("(kc p) n -> p kc n", p=128)))
                for dcp in range(4):
                    src = w_down[l][fh * 2048:(fh + 1) * 2048, :].rearrange("(fc p) (q c) -> p fc q c", p=128, q=4)[:, :, dcp, :]
                    P.append((f"DN{fh}_{dcp}", [128, 16, 256], src))
            return P

        plan = []
        for h in range(2):
            for l in range(DEPTH):
                plan += [(f"h{h}l{l}{n}", shp, src) for (n, shp, src) in wplan(l)]
        rstate = {'issued': 0, 'next': 0}

        def ring_view(i, shp):
            slot = i % self.NSLOT
            n = int(np.prod(shp[1:]))
            v = ring[slot][:, 0:n]
            if len(shp) == 3:
                v = v.rearrange("p (a b) -> p a b", a=shp[1])
            else:
                v = v.rearrange("p (a b c) -> p a b c", a=shp[1], b=shp[2])
            return v

        def ring_issue_upto(i):
            while rstate['issued'] <= min(i, len(plan) - 1):
                j = rstate['issued']
                name, shp, src = plan[j]
                slot = j % self.NSLOT
                rv = ring_view(j, shp)
                if len(shp) == 4 and not name[4:].startswith('BR'):
                    for q in range(shp[2]):
                        dma('pool', rv[:, :, q, :], src[:, :, q, :], ring_sem[slot], R=[], Wr=[('ring', slot)], group=grp)
                    self.commit(grp)
                else:
                    dma('pool', rv, src, ring_sem[slot], R=[], Wr=[('ring', slot)])
                rstate['issued'] += 1

        def wnext(name, hold=0):
            i = rstate['next']
            assert plan[i][0].endswith(name), (plan[i][0], name)
            ring_issue_upto(i + self.NSLOT - 1 - hold)
            rstate['next'] += 1
            return ring_view(i, plan[i][1]), ('ring', i % self.NSLOT)

        ring_issue_upto(self.NSLOT - 2)

        self.ck(0)
        for h in range(2):
            self.barrier()
            self.ck(10)
            cv.reset(SCR0)
            xin = cv.get([128, 2, D], F32)
            for m in range(NM + 1):
                sl = m % 2
                rows = 128 if m < NM else S
                c0 = m * 128 if m < NM else PH
                src = x_prompt[h * PH + m * 128:h * PH + m * 128 + 128, :] if m < NM else x_sample[h * S:(h + 1) * S, :]
                dma('sp', xin[0:rows, sl, :], src, dsem_x[sl], Wr=[('xin', sl)])
                self.ck(11)
                for kg in range(2):
                    pb, pk = bank()
                    for i in range(4):
                        kc = kg * 4 + i
                        op('pe', lambda i=i, kc=kc: nc.tensor.transpose(out=pb[:, i * 128:i * 128 + rows],
                                                                        in_=xin[0:rows, sl, kc * 128:(kc + 1) * 128],
                                                                        identity=ident_f[0:rows, 0:rows]),
                           R=[('xin', sl), 'ident_f'], Wr=[pk])
                    self.ck(12)
                    pv = pb[:, :].rearrange("p (a b) -> p a b", a=4)[:, :, 0:rows]
                    keys = [('R', kg * 4 + i, c0 // 512 if m < NM else len(tts) - 1) for i in range(4)]
                    xkeys = [('xbf', kg * 4 + i, c0 // 512 if m < NM else len(tts) - 1) for i in range(4)]
                    op('act', lambda: nc.scalar.mul(out=Rr[:, kg * 4:kg * 4 + 4, c0:c0 + rows], in_=pv, mul=ALPHA),
                       R=[pk], Wr=keys)
                    self.ck(13)
                    op('dve', lambda: nc.vector.tensor_copy(out=xbf[:, kg * 4:kg * 4 + 4, c0:c0 + rows], in_=pv),
                       R=[pk], Wr=xkeys)
                    self.ck(14)
                self.ck(15)
                if m == NM - 1:
                    self.ck(16)

            self.ck(1)

            def Rk(c, ti):
                return ('R', c, ti)

            def Xk(c, ti):
                return ('xbf', c, ti)

            def allX(ti):
                return [Xk(c, ti) for c in range(8)]

            def tt_of_tile(m):
                return (m * 128) // 512 if m < NM else len(tts) - 1

            for l in range(DEPTH):
                snap = None
                if l == 0:
                    self.barrier()
                else:
                    snap = self.snapshot()
                self.new_phase()
                last = (l == DEPTH - 1)
                b0 = h * NBH

                def layer_prep_dma(h_, l_):
                    l = l_
                    b0 = h_ * NBH
                    nb2 = NBH // 2
                    dma('sp', wsTf[:], w_s[l].rearrange("g t s -> t g s"), dsem_lay, Wr=['wsTf'], group=grp)
                    dma('sp', bs_row[:], b_s[l:l + 1], dsem_lay, Wr=['bs_row'], group=grp)
                    dma('sp', bsS_row[:], b_s[l:l + 1, :, 0:4].unsqueeze(2).broadcast_to([1, 4, NBH, 4]), dsem_lay, Wr=['bsS_row'], group=grp)
                    dma('sp', lnv_g[:], ln_v_g[l].partition_broadcast(128), dsem_lay, Wr=['lnv_g'], group=grp)
                    dma('sp', lnv_b[:], ln_v_b[l].partition_broadcast(128), dsem_lay, Wr=['lnv_b'], group=grp)
                    dma('sp', tailsT[0:2 * NBH, :], state_a[l, 2 * b0:2 * b0 + 2 * NBH, :], dsem_lay, R=['tailsT'], Wr=[('st_stage', 0), 'tailsT'], group=grp)
                    for q in range(2):
                        dma('sp', st_stage[0:nb2 * 30, q, :],
                            state_c[l, b0 + q * nb2:b0 + (q + 1) * nb2].rearrange("b k c -> (b k) c"), dsem_lay, Wr=[('st_stage', 1 + q)], group=grp)
                    self.commit(grp)
                    dma('sp', o_cc_s[l, b0:b0 + NBH, 0:26, :], state_c[l, b0:b0 + NBH, 4:30, :], dsem_d2d)


                def layer_prep_compute(h_, l_):
                    l = l_
                    b0 = h_ * NBH
                    nb2 = NBH // 2
                    pb, pk = bank()
                    for g in range(4):
                        op('pe', lambda g=g: nc.tensor.transpose(out=pb[:, g * 128:(g + 1) * 128], in_=wsTf[:, g, :],
                                                                 identity=ident_f[:]),
                           R=['wsTf', 'ident_f'], Wr=[pk])
                    op('dve', lambda: nc.vector.tensor_copy(out=wsTf[:].rearrange("p a b -> p (a b)"), in_=pb[:, :]),
                       R=[pk], Wr=['wsTf'])
                    for bi in range(NBH):
                        dma('sp', wblkf[4 * bi:4 * bi + 4, :, 4 * bi:4 * bi + 4], wsTf[0:4, :, 0:4], dsem_st,
                            R=['wsTf', 'wblkf'], Wr=[('wblkf', bi)], group=grp)
                    self.commit(grp)
                    for g in range(4):
                        op('pool', lambda g=g: nc.gpsimd.affine_select(out=wsTf[:, g, :], in_=wsTf[:, g, :],
                                                                       compare_op=ALU.is_ge, fill=0.0, base=0,
                                                                       pattern=[[1, 128]], channel_multiplier=-1),
                           R=['wsTf', 'wblkf'] + [('wblkf', bi) for bi in range(NBH)], Wr=['wsTf'])
                        op('pool', lambda g=g: nc.gpsimd.affine_select(out=wblkf[:, g, :], in_=wblkf[:, g, :],
                                                                       compare_op=ALU.is_ge, fill=0.0, base=0,
                                                                       pattern=[[1, 32]], channel_multiplier=-1),
                           R=['wblkf'], Wr=['wblkf'] + [('wblkf', bi) for bi in range(NBH)])
                    op('dve', lambda: nc.vector.tensor_copy(out=wsT[:], in_=wsTf[:]), R=['wsTf'], Wr=['wsT'])
                    op('dve', lambda: nc.vector.tensor_copy(out=wblk[:], in_=wblkf[:]), R=['wblkf'], Wr=['wblk'])

                    pb, pk = bank()
                    for j in range(4):
                        op('pe', lambda j=j: nc.tensor.transpose(out=pb[:, j * 128:j * 128 + 2 * NBH],
                                                                 in_=tailsT[0:2 * NBH, j * 128:(j + 1) * 128],
                                                                 identity=ident_f[0:2 * NBH, 0:2 * NBH]),
                           R=[('st_stage', 0), 'tailsT', 'ident_f'], Wr=[pk])
                    op('act', lambda: nc.scalar.copy(
                        out=p_s[:, :, :, 0:2],
                        in_=pb[:, :].rearrange("p (j r) -> p j r", j=4)[:, :, 0:2 * NBH].rearrange("p j (b k) -> p j b k", k=2)),
                       R=[pk], Wr=['p_s'])
                    for q in range(2):
                        pb, pk = bank()
                        nr = nb2 * 30
                        for j in range(4):
                            op('pe', lambda j=j: nc.tensor.transpose(out=pb[:, j * 128:j * 128 + nr],
                                                                     in_=st_stage[0:nr, q, j * 128:(j + 1) * 128],
                                                                     identity=ident_f[0:nr, 0:nr]),
                               R=[('st_stage', 1 + q), 'ident_f'], Wr=[pk])
                        op('act', lambda: nc.scalar.copy(
                            out=glu_s[:, :, q * nb2:(q + 1) * nb2, 0:30],
                            in_=pb[:, :].rearrange("p (j r) -> p j r", j=4)[:, :, 0:nr].rearrange("p j (b k) -> p j b k", k=30)),
                           R=[pk], Wr=['glu_s'])


                if h == 0 and l == 0:
                    layer_prep_dma(0, 0)
                    layer_prep_compute(0, 0)
                nxt = (h, l + 1) if l + 1 < DEPTH else ((h + 1, 0) if h == 0 else None)

                self.ck(2)
                cv.reset(SCR0)
                p_pr = cv.get([128, 2, 2 + PH], F32)
                xa_sb = cv.get([128, 2, 512], F32)
                acc_a = cv.get([128, 2, 512], F32)
                u_sb = cv.get([128, 4, T], BF16)
                vg = cv.get([128, 4, 512], F32)
                vbf = cv.get([128, 4, 512], BF16)
                bst = cv.get([128, 4, 8], F32)

                i_xa = [0]

                def a_init(j):
                    sl = j % 2
                    if h == 0:
                        op('dve', lambda: nc.vector.memset(p_pr[:, sl, 0:2], 0.0), R=['wodone'], Wr=[('p_pr', sl)])
                    else:
                        op('dve', lambda: nc.vector.tensor_copy(out=p_pr[:, sl, 0:2], in_=pa_hist[:, l, j, :]),
                           R=[('pa_hist', l, j), 'wodone'], Wr=[('p_pr', sl)])

                def a_body(j, wt, wk, ti, s0, n):
                    sl = j % 2
                    smp = (ti == len(tts) - 1)
                    bx, kx = bank()
                    bg, kg_ = bank()
                    bc, kc_ = bank()
                    for seg, (bb, kk) in enumerate(((bx, kx), (bg, kg_), (bc, kc_))):
                        mm(bb[:, 0:n], [(wt[:, kc, seg, 0:128], xbf[:, kc, s0:s0 + n]) for kc in range(8)],
                           R=[wk] + allX(ti), Wr=[kk])
                    xs = i_xa[0] % 2
                    i_xa[0] += 1
                    op('act', lambda: nc.scalar.copy(out=xa_sb[:, xs, 0:n], in_=bx[:, 0:n]), R=[kx, 'wodone'], Wr=[('xa_sb', xs)])
                    wcol = lambda k: pT[:, l, WA + k * 4 + j:WA + k * 4 + j + 1]
                    if not smp:
                        pdst = p_pr[:, sl, 2 + s0:2 + s0 + n]
                        op('dve', lambda: nc.vector.tensor_tensor(out=pdst, in0=bc[:, 0:n], in1=xa_sb[:, xs, 0:n], op=ALU.mult),
                           R=[kc_, ('xa_sb', xs), 'wodone'], Wr=[('p_pr', sl)])
                        srcs = [p_pr[:, sl, s0 + k:s0 + k + n] for k in range(3)]
                        accv = acc_a[:, xs, 0:n]
                        gav = bg[:, 0:n]
                        yav = ya[:, j, s0:s0 + n]
                        pkey = ('p_pr', sl)
                    else:
                        pdst = p_s[:, j, :, 2:6]
                        op('dve', lambda: nc.vector.tensor_tensor(
                            out=pdst, in0=bc[:, 0:n].rearrange("p (b k) -> p b k", k=4),
                            in1=xa_sb[:, xs, 0:n].rearrange("p (b k) -> p b k", k=4), op=ALU.mult),
                           R=[kc_, ('xa_sb', xs)], Wr=['p_s'])
                        srcs = [p_s[:, j, :, k:k + 4] for k in range(3)]
                        accv = acc_a[:, xs, 0:n].rearrange("p (b k) -> p b k", k=4)
                        gav = bg[:, 0:n].rearrange("p (b k) -> p b k", k=4)
                        yav = ya[:, j, s0:s0 + n].rearrange("p (b k) -> p b k", k=4)
                        pkey = 'p_s'
                    op('dve', lambda: nc.vector.tensor_scalar(out=accv, in0=srcs[0], scalar1=wcol(0), scalar2=None, op0=ALU.mult),
                       R=[pkey, ('pT', l), 'wodone'], Wr=[('acc_a', xs)])
                    for k in (1, 2):
                        op('dve', lambda k=k: nc.vector.scalar_tensor_tensor(out=accv, in0=srcs[k], scalar=wcol(k), in1=accv,
                                                                             op0=ALU.mult, op1=ALU.add),
                           R=[pkey, ('pT', l), ('acc_a', xs)], Wr=[('acc_a', xs)])
                    op('dve', lambda: nc.vector.tensor_tensor(out=yav, in0=gav, in1=accv, op=ALU.mult),
                       R=[kg_, ('acc_a', xs), 'adone'], Wr=[('ya', j, ti)])

                def a_tail(j):
                    sl = j % 2
                    if h == 0:
                        op('act', lambda: nc.scalar.copy(out=pa_hist[:, l, j, :], in_=p_pr[:, sl, PH:PH + 2]),
                           R=[('p_pr', sl)], Wr=[('pa_hist', l, j)])
                    else:
                        op('act', lambda: nc.scalar.copy(out=tails[:, j, TA_P:TA_P + 2], in_=p_pr[:, sl, PH:PH + 2]),
                           R=[('p_pr', sl)], Wr=['tails'])
                    op('act', lambda: nc.scalar.copy(out=tails[:, j, TA_S:TA_S + 2 * NBH].rearrange("p (b k) -> p b k", k=2),
                                                     in_=p_s[:, j, :, 4:6]),
                       R=['p_s'], Wr=['tails'])


                for jp in range(2):
                    wtA, wkA = wnext(f"A{2 * jp}")
                    wtB, wkB = wnext(f"A{2 * jp + 1}", hold=1)
                    a_init(2 * jp)
                    a_init(2 * jp + 1)
                    for ti, (s0, n) in enumerate(tts):
                        a_body(2 * jp, wtA, wkA, ti, s0, n)
                        a_body(2 * jp + 1, wtB, wkB, ti, s0, n)
                    a_tail(2 * jp)
                    a_tail(2 * jp + 1)

                self.ck(3)
                if snap is not None:
                    self.apply_snapshot(snap)
                wt, wk = wnext("U")
                for j in range(4):
                    for ti, (s0, n) in enumerate(tts):
                        bb, kk = bank()
                        mm(bb[:, 0:n], [(wt[:, kc, j * 128:(j + 1) * 128], xbf[:, kc, s0:s0 + n]) for kc in range(8)],
                           R=[wk] + allX(ti), Wr=[kk])
                        op('act', lambda: nc.scalar.activation(out=u_sb[:, j, s0:s0 + n], in_=bb[:, 0:n], func=AF.Gelu),
                           R=[kk], Wr=[('u', j, ti)])
                wt, wk = wnext("V")
                NSB = 4

                def bstage1(m):
                    smp = (m == NM)
                    rows = S if smp else 128
                    c0 = PH if smp else m * 128
                    ti = tt_of_tile(m)
                    sl = m % NSB
                    bv, kv = bank()
                    mm(bv[0:rows, 0:512], [(xbf[:, kc, c0:c0 + rows], wt[:, kc, 0:512]) for kc in range(8)],
                       R=[wk] + allX(ti), Wr=[kv])
                    op('act', lambda: nc.scalar.activation(out=vg[0:rows, sl, :], in_=bv[0:rows, 0:512], func=AF.Gelu),
                       R=[kv], Wr=[('vg', sl)])
                    op('dve', lambda: nc.vector.bn_stats(out=bst[0:rows, sl, 0:6], in_=vg[0:rows, sl, :]),
                       R=[('vg', sl)], Wr=[('bst', sl)])
                    op('dve', lambda: nc.vector.bn_aggr(out=bst[0:rows, sl, 6:8], in_=bst[0:rows, sl, 0:6]),
                       R=[('bst', sl)], Wr=[('bst', sl)])
                    op('dve', lambda: nc.vector.tensor_scalar(out=bst[0:rows, sl, 7:8], in0=bst[0:rows, sl, 7:8],
                                                              scalar1=EPS, scalar2=None, op0=ALU.add),
                       R=[('bst', sl)], Wr=[('bst', sl)])
                    op('pool', lambda: nc.gpsimd.tensor_tensor(out=bst[0:rows, sl, 7:8], in0=bst[0:rows, sl, 7:8],
                                                               in1=neghalf[0:rows, 0:1], op=ALU.pow),
                       R=[('bst', sl), 'neghalf'], Wr=[('bst', sl)])
                    op('dve', lambda: nc.vector.tensor_scalar(out=vg[0:rows, sl, :], in0=vg[0:rows, sl, :],
                                                              scalar1=bst[0:rows, sl, 6:7], scalar2=bst[0:rows, sl, 7:8],
                                                              op0=ALU.subtract, op1=ALU.mult),
                       R=[('bst', sl), ('vg', sl)], Wr=[('vg', sl)])
                    op('dve', lambda: nc.vector.tensor_tensor(out=vg[0:rows, sl, :], in0=vg[0:rows, sl, :], in1=lnv_g[0:rows, :], op=ALU.mult),
                       R=[('vg', sl), 'lnv_g'], Wr=[('vg', sl)])
                    if smp:
                        op('dve', lambda: nc.vector.tensor_tensor(out=vg[0:rows, sl, :], in0=vg[0:rows, sl, :], in1=lnv_b[0:rows, :], op=ALU.add),
                           R=[('vg', sl), 'lnv_b'], Wr=[('vg', sl)])
                        dma('sp', o_v_s[l, 4 * b0:4 * b0 + S, :], vg[0:rows, sl, :], dsem_vo, R=[('vg', sl)])
                        op('dve', lambda: nc.vector.tensor_copy(out=vbf[0:rows, sl, :], in_=vg[0:rows, sl, :]),
                           R=[('vg', sl)], Wr=[('vbf', sl)])
                    else:
                        op('dve', lambda: nc.vector.tensor_tensor(out=vbf[0:rows, sl, :], in0=vg[0:rows, sl, :], in1=lnv_b[0:rows, :], op=ALU.add),
                           R=[('vg', sl), 'lnv_b'], Wr=[('vbf', sl)])

                def bstage2(m):
                    smp = (m == NM)
                    rows = S if smp else 128
                    c0 = PH if smp else m * 128
                    ti = tt_of_tile(m)
                    sl = m % NSB
                    bs_, ks_ = bank()
                    ncol = rows
                    for g in range(4):
                        if smp:
                            prs = [(vbf[0:rows, sl, g * 128:(g + 1) * 128], wblk[0:rows, g, :]),
                                   (ones_row[0:1, :], bsS_row[0:1, g, :, :].rearrange("p b k -> p (b k)"))]
                        else:
                            prs = [(vbf[0:rows, sl, g * 128:(g + 1) * 128], wsT[:, g, :]),
                                   (ones_row[0:1, :], bs_row[0:1, g, :])]
                        mm(bs_[:, g * ncol:(g + 1) * ncol], prs,
                           R=[('vbf', sl), 'wsT', 'wblk', 'bs_row', 'bsS_row', 'ones_row'], Wr=[ks_])
                    op('dve', lambda: nc.vector.tensor_tensor(out=yb[:, :, c0:c0 + ncol],
                                                              in0=bs_[:, 0:4 * ncol].rearrange("p (g t) -> p g t", g=4),
                                                              in1=u_sb[:, :, c0:c0 + ncol], op=ALU.mult),
                       R=[ks_] + [('u', j, ti) for j in range(4)], Wr=[('yb', m)])

                for m in range(NM + 1 + 2):
                    if m < NM + 1:
                        bstage1(m)
                    if m >= 2:
                        bstage2(m - 2)

                self.ck(4)
                self.barrier()
                cv.reset(SCR0)
                kT = cv.get([128, 4, NMEM], BF16)
                vP = cv.get([128, 2, W], BF16)
                glu = cv.get([128, 2, 30 + PH], BF16)
                glu_sb = cv.get([128, NBH, 34], BF16)
                dg = cv.get([128, 31, 128], BF16)
                cc = cv.get([128, 4, T], F32)
                sqt = cv.get([128, 2, 512], BF16)
                lnt = cv.get([128, 3, T], F32)
                t1b = cv.get([128, 2, 512], F32)
                sig = t1b

                i_x = 0
                tP = len(tts) - 2
                for j in range(4):
                    if j % 2 == 0:
                        wt, wk = wnext(f"C{j // 2}")
                    jj = j % 2
                    sl = j % 2
                    wc = lambda k: pT[:, l, WC + k * 4 + j:WC + k * 4 + j + 1]
                    bcol = pT[:, l, BC + j:BC + j + 1]
                    for k in range(31):
                        op('dve', lambda k=k: nc.vector.tensor_scalar(out=dg[:, k, :], in0=ident_b[:], scalar1=wc(k), scalar2=None, op0=ALU.mult),
                           R=['ident_b', ('pT', l)], Wr=[('dg', k)])
                    if h == 0:
                        op('dve', lambda: nc.vector.memset(glu[:, sl, 0:30], 0.0), Wr=[('glu', sl)])
                    else:
                        op('dve', lambda: nc.vector.tensor_copy(out=glu[:, sl, 0:30], in_=glu_hist[:, l, j, :]),
                           R=[('glu_hist', l, j)], Wr=[('glu', sl)])
                    for ti, (s0, n) in enumerate(tts):
                        smp = (ti == len(tts) - 1)
                        ba, ka = bank()
                        bb, kb = bank()
                        mm(ba[:, 0:n], [(wt[:, kc, 0, jj * 128:(jj + 1) * 128], xbf[:, kc, s0:s0 + n]) for kc in range(8)],
                           R=[wk] + allX(ti), Wr=[ka])
                        mm(bb[:, 0:n], [(wt[:, kc, 1, jj * 128:(jj + 1) * 128], xbf[:, kc, s0:s0 + n]) for kc in range(8)],
                           R=[wk] + allX(ti), Wr=[kb])
                        xs = i_x % 2
                        i_x += 1
                        op('act', lambda: nc.scalar.activation(out=sig[:, xs, 0:n], in_=bb[:, 0:n], func=AF.Sigmoid),
                           R=[kb], Wr=[('t1b', xs)])
                        if not smp:
                            op('dve', lambda: nc.vector.tensor_tensor(out=glu[:, sl, 30 + s0:30 + s0 + n], in0=ba[:, 0:n],
                                                                      in1=sig[:, xs, 0:n], op=ALU.mult),
                               R=[ka, ('t1b', xs)], Wr=[('glu', sl)])
                            if ti == tP:
                                tdst = glu_hist[:, l, j, :] if h == 0 else tails[:, j, TC_P:TC_P + 30]
                                tkey = ('glu_hist', l, j) if h == 0 else 'tails'
                                op('dve', lambda: nc.vector.tensor_tensor(out=tdst, in0=ba[:, n - 30:n], in1=sig[:, xs, n - 30:n], op=ALU.mult),
                                   R=[ka, ('t1b', xs)], Wr=[tkey])
                        else:
                            op('dve', lambda: nc.vector.tensor_tensor(out=glu_s[:, j, :, 30:34],
                                                                      in0=ba[:, 0:n].rearrange("p (b k) -> p b k", k=4),
                                                                      in1=sig[:, xs, 0:n].rearrange("p (b k) -> p b k", k=4), op=ALU.mult),
                               R=[ka, ('t1b', xs)], Wr=['glu_s'])
                            op('act', lambda: nc.scalar.copy(out=glu_sb[:, :, :], in_=glu_s[:, j, :, :]), R=['glu_s'], Wr=['glu_sb'])
                            op('act', lambda: nc.scalar.copy(out=tails[:, j, TC_S:TC_S + 4 * NBH].rearrange("p (b k) -> p b k", k=4),
                                                             in_=glu_s[:, j, :, 30:34]),
                               R=['glu_s'], Wr=['tails'])
                    dgk = [('dg', k) for k in range(31)]
                    for ti, (s0, n) in enumerate(tts):
                        smp = (ti == len(tts) - 1)
                        bk_, kk_ = bank()
                        if not smp:
                            mm(bk_[:, 0:n], [(dg[:, k, :], glu[:, sl, s0 + k:s0 + k + n]) for k in range(31)],
                               R=dgk + [('glu', sl)], Wr=[kk_])
                        else:
                            mm(bk_[:, 0:n].rearrange("p (b k) -> p b k", k=4), [(dg[:, k, :], glu_sb[:, :, k:k + 4]) for k in range(31)],
                               R=dgk + ['glu_sb'], Wr=[kk_])
                        op('act', lambda: nc.scalar.activation(out=cc[:, j, s0:s0 + n], in_=bk_[:, 0:n], func=AF.Identity, bias=bcol),
                           R=[kk_, ('pT', l)], Wr=[('cc', j, ti)])

                pb, pk = bank()
                for j in range(4):
                    op('pe', lambda j=j: nc.tensor.transpose(out=pb[0:NT, j * 128:(j + 1) * 128], in_=tails[:, j, :], identity=ident_f[:]),
                       R=['tails', 'ident_f'], Wr=[pk])
                op('act', lambda: nc.scalar.copy(out=tailsT[0:NT, :], in_=pb[0:NT, :]), R=[pk], Wr=['tailsT'])
                if h == 1:
                    dma('sp', o_ca_p[l], tailsT[TA_P:TA_P + 2, :], dsem_tail, R=['tailsT'])
                    dma('sp', o_cc_p[l], tailsT[TC_P:TC_P + 30, :], dsem_tail, R=['tailsT'])
                dma('sp', o_ca_s[l, 2 * b0:2 * b0 + 2 * NBH, :], tailsT[TA_S:TA_S + 2 * NBH, :], dsem_tail, R=['tailsT'])
                for bi in range(NBH):
                    dma('sp', o_cc_s[l, b0 + bi, 26:30, :], tailsT[TC_S + 4 * bi:TC_S + 4 * bi + 4, :], dsem_tail, R=['tailsT'])

                def cck(j, ti):
                    return ('cc', j, ti)

                def ln_acc(srcf, keyf, nch, ones_m, n):
                    bm, km = bank(pin=True)
                    be, ke = bank(pin=True)
                    for c in range(nch):
                        xs = c % 2
                        op('act', lambda c=c, xs=xs: nc.scalar.activation(out=sqt[:, xs, 0:n], in_=srcf(c), func=AF.Square),
                           R=[keyf(c)], Wr=[('sqt', xs)])
                        op('pe', lambda c=c: nc.tensor.matmul(bm[:, 0:n], ones_m[:], srcf(c), start=(c == 0), stop=(c == nch - 1)),
                           R=[keyf(c), 'onesW', 'onesD'], Wr=[km])
                        op('pe', lambda c=c, xs=xs: nc.tensor.matmul(be[:, 0:n], (onesWb if ones_m is onesW else onesDb)[:], sqt[:, xs, 0:n],
                                                                     start=(c == 0), stop=(c == nch - 1)),
                           R=[('sqt', xs), 'onesWb', 'onesDb'], Wr=[ke])
                    return (bm, km, be, ke)

                def ln_fin(acc, lo, n, kt):
                    bm, km, be, ke = acc
                    L0, L1, L2 = lnt[:, 0, lo:lo + n], lnt[:, 1, lo:lo + n], lnt[:, 2, lo:lo + n]
                    op('act', lambda: nc.scalar.activation(out=L0, in_=bm[:, 0:n], func=AF.Square), R=[km], Wr=[('lnt', 0, kt)])
                    op('dve', lambda: nc.vector.scalar_tensor_tensor(out=L0, in0=be[:, 0:n], scalar=EPS, in1=L0,
                                                                     op0=ALU.add, op1=ALU.subtract),
                       R=[ke, ('lnt', 0, kt)], Wr=[('lnt', 0, kt)])
                    op('act', lambda: nc.scalar.activation(out=L0, in_=L0, func=AF.Ln),
                       R=[('lnt', 0, kt)], Wr=[('lnt', 0, kt)])
                    op('act', lambda: nc.scalar.activation(out=L1, in_=L0, func=AF.Exp, scale=-0.5),
                       R=[('lnt', 0, kt)], Wr=[('lnt', 1, kt)])
                    op('dve', lambda: nc.vector.scalar_tensor_tensor(out=L2, in0=bm[:, 0:n], scalar=-1.0, in1=L1,
                                                                     op0=ALU.mult, op1=ALU.mult),
                       R=[km, ('lnt', 1, kt)], Wr=[('lnt', 2, kt)])
                    self.unpin(km)
                    self.unpin(ke)

                def ln_apply(buf, nch, keys, s0, n, lo, kt):
                    for c0 in range(0, nch, 4):
                        c1 = min(nch, c0 + 4)
                        gk = keys[c0:c1]
                        v = buf[:, c0:c1, s0:s0 + n]
                        rb = lnt[:, 1, lo:lo + n].unsqueeze(1).broadcast_to([128, c1 - c0, n])
                        op('dve', lambda: nc.vector.tensor_tensor(out=v, in0=v, in1=rb, op=ALU.mult),
                           R=[('lnt', 1, kt)] + gk, Wr=gk)
                        npool = 0
                        nd = (c1 - c0) - npool
                        v1 = buf[:, c0:c0 + nd, s0:s0 + n]
                        nb1 = lnt[:, 2, lo:lo + n].unsqueeze(1).broadcast_to([128, nd, n])
                        op('dve', lambda: nc.vector.tensor_tensor(out=v1, in0=v1, in1=nb1, op=ALU.add),
                           R=[('lnt', 2, kt)] + gk[:nd], Wr=gk[:nd])
                        if npool:
                            v2 = buf[:, c0 + nd:c1, s0:s0 + n]
                            nb2_ = lnt[:, 2, lo:lo + n].unsqueeze(1).broadcast_to([128, npool, n])
                            op('pool', lambda: nc.gpsimd.tensor_tensor(out=v2, in0=v2, in1=nb2_, op=ALU.add),
                               R=[('lnt', 2, kt)] + gk[nd:], Wr=gk[nd:])

                def c_acc(ti):
                    s0, n = tts[ti]
                    return ln_acc(lambda c: cc[:, c, s0:s0 + n], lambda c: cck(c, ti), 4, onesW, n)

                def c_app(ti):
                    s0, n = tts[ti]
                    ln_apply(cc, 4, [cck(j, ti) for j in range(4)], s0, n, s0, ti)
                    for j in range(4):
                        op('act', lambda j=j: nc.scalar.activation(out=yc[:, j, s0:s0 + n], in_=cc[:, j, s0:s0 + n], func=AF.Silu,
                                                                   bias=pcol(l, LCB + j), scale=pcol(l, LCG + j)),
                           R=[cck(j, ti), ('pT', l)], Wr=[('yc', j, ti)])
                def kv_proj():
                    kvout = t1b
                    wkk, kkk = wnext("KVK")
                    for h4 in range(4):
                        bb, kk = bank()
                        mm(bb[:, 0:NMEM], [(wkk[:, kc, h4 * 128:(h4 + 1) * 128], memT[:, kc, :]) for kc in range(8)],
                           R=[kkk, 'memT'], Wr=[kk])
                        op('act', lambda: nc.scalar.copy(out=kT[:, h4, :], in_=bb[:, 0:NMEM]), R=[kk], Wr=['kT'])
                    i_o = 0
                    for isv in (False, True):
                        if isv:
                            wv_, kv_ = wnext("KVV")
                            dst = o_mv
                        else:
                            wv_, kv_, dst = wkk, kkk, o_mk
                        for mc in range(2):
                            bb, kk = bank()
                            mm(bb[:, 0:W], [(memT[:, kc, mc * 128:(mc + 1) * 128], wv_[:, kc, :]) for kc in range(8)],
                               R=[kv_, 'memT'], Wr=[kk])
                            sl = i_o % 2
                            i_o += 1
                            op('dve', lambda: nc.vector.tensor_copy(out=kvout[:, sl, :], in_=bb[:, 0:W]), R=[kk], Wr=[('t1b', sl)])
                            if isv:
                                op('act', lambda: nc.scalar.copy(out=vP[:, mc, :], in_=bb[:, 0:W]), R=[kk], Wr=['vP'])
                            dma('sp', dst[l, mc * 128:(mc + 1) * 128, :], kvout[:, sl, :], dsem_kvo[sl], R=[('t1b', sl)])

                nt_ = len(tts)
                caccs = {}
                for step in range(nt_ + 2):
                    if step == nt_:
                        kv_proj()
                    if step < nt_:
                        caccs[step] = c_acc(step)
                    if 0 <= step - 1 < nt_:
                        ln_fin(caccs[step - 1], tts[step - 1][0], tts[step - 1][1], step - 1)
                    if 0 <= step - 2 < nt_:
                        c_app(step - 2)

                self.ck(5)
                self.barrier()
                cv.reset(SCR0)
                kT = cv.get([128, 4, NMEM], BF16)
                vP = cv.get([128, 2, W], BF16)
                qT = cv.get([128, 4, T], BF16)
                e_bf = cv.get([128, 2, 2, 512], BF16)
                rden = cv.get([128, 2, 512], F32)
                NKS = 4
                kS_raw = cv.get([128, NKS, 2, W], BF16)
                kTs = cv.get([128, NKS, 4, NMEM], BF16)
                vS = cv.get([128, 2, 2, W], BF16)
                e_s = cv.get([128, 8 * S], BF16)
                rden_s = cv.get([128, 4 * S], F32)

                def issue_ks(bi):
                    sl = bi % NKS
                    dma('pool', kS_raw[:, sl, :, :], cache_k[l, b0 + bi].rearrange("(mc p) c -> p mc c", p=128), dsem_ks[sl],
                        Wr=[('kS_raw', sl)])

                def vloc(bi):
                    if NBH == 8 and 2 <= bi <= 5:
                        sl = bi - 2
                        return kS_raw[:, sl, :, :], ('kS_raw', sl), dsem_ks[sl]
                    sl = bi % 2
                    return vS[:, sl, :, :], ('vS', sl), dsem_vs[sl]

                def issue_vs(bi):
                    buf, key, ds = vloc(bi)
                    dma('pool', buf, cache_v[l, b0 + bi].rearrange("(mc p) c -> p mc c", p=128), ds, Wr=[key])
                for bi_ in range(min(NKS, NBH)):
                    issue_ks(bi_)
                issue_vs(0)
                issue_vs(1)
                wq, kq = wnext("Q")
                for h4 in range(4):
                    for ti, (s0, n) in enumerate(tts):
                        bb, kk = bank()
                        mm(bb[:, 0:n], [(wq[:, kc, h4 * 128:(h4 + 1) * 128], xbf[:, kc, s0:s0 + n]) for kc in range(8)],
                           R=[kq] + allX(ti), Wr=[kk])
                        op('act', lambda: nc.scalar.copy(out=qT[:, h4, s0:s0 + n], in_=bb[:, 0:n]), R=[kk], Wr=[('qT', h4, ti)])
                i_x = 0
                for ti, (s0, n) in enumerate(tts[:-1]):
                    for h4 in range(4):
                        xs = i_x % 2
                        i_x += 1
                        bsc = []
                        for mc in range(2):
                            bb, kk = bank()
                            mm(bb[:, 0:n], [(kT[:, h4, mc * 128:(mc + 1) * 128], qT[:, h4, s0:s0 + n])], R=['kT', ('qT', h4, ti)], Wr=[kk])
                            bsc.append((bb, kk))
                        for mc in range(2):
                            bb, kk = bsc[mc]
                            op('act', lambda mc=mc, bb=bb: nc.scalar.activation(out=e_bf[:, xs, mc, 0:n], in_=bb[:, 0:n], func=AF.Exp, scale=QSCALE),
                               R=[kk], Wr=[('e_bf', xs, mc)])
                        bd, kd = bank()
                        mm(bd[:, 0:n], [(ones_b[:], e_bf[:, xs, mc, 0:n]) for mc in range(2)], R=[('e_bf', xs, 0), ('e_bf', xs, 1), 'ones_b'], Wr=[kd])
                        bp, kp = bank()
                        mm(bp[:, 0:n], [(vP[:, mc, h4 * 128:(h4 + 1) * 128], e_bf[:, xs, mc, 0:n]) for mc in range(2)],
                           R=[('e_bf', xs, 0), ('e_bf', xs, 1), 'vP'], Wr=[kp])
                        op('act', lambda: nc.scalar.activation(out=rden[:, xs, 0:n], in_=bd[:, 0:n], func=AF.Ln), R=[kd], Wr=[('rden', xs)])
                        op('act', lambda: nc.scalar.activation(out=rden[:, xs, 0:n], in_=rden[:, xs, 0:n], func=AF.Exp, scale=-1.0),
                           R=[('rden', xs)], Wr=[('rden', xs)])
                        op('dve', lambda: nc.vector.tensor_tensor(out=yx[:, h4, s0:s0 + n], in0=bp[:, 0:n], in1=rden[:, xs, 0:n], op=ALU.mult),
                           R=[kp, ('rden', xs)], Wr=[('yx', h4, ti)])
                tS = len(tts) - 1
                bsS, ksS = bank(pin=True)

                def ks_stage1(bi):
                    sl = bi % NKS
                    ptb, ptk = bank()
                    ptv = ptb[:, :].bitcast(BF16)
                    for h4 in range(4):
                        for mc in range(2):
                            o0 = (h4 * 2 + mc) * 128
                            op('pe', lambda h4=h4, mc=mc, o0=o0: nc.tensor.transpose(out=ptv[:, o0:o0 + 128],
                                                                                     in_=kS_raw[:, sl, mc, h4 * 128:(h4 + 1) * 128],
                                                                                     identity=ident_b[:]),
                               R=[('kS_raw', sl), 'ident_b'], Wr=[ptk])
                    op('dve', lambda: nc.vector.tensor_copy(out=kTs[:, sl, :, :].rearrange("p a b -> p (a b)"), in_=ptv[:, :]),
                       R=[ptk], Wr=[('kTs', sl)])
                    if bi + NKS < NBH:
                        issue_ks(bi + NKS)
                    elif NBH == 8:
                        issue_vs(bi - 2)

                def ks_stage2(bi):
                    sl = bi % NKS
                    for h4 in range(4):
                        for mc in range(2):
                            o0 = ((h4 * 2 + mc) * NBH + bi) * 4
                            mm(bsS[:, o0:o0 + 4], [(kTs[:, sl, h4, mc * 128:(mc + 1) * 128], qT[:, h4, PH + 4 * bi:PH + 4 * bi + 4])],
                               R=[('kTs', sl), ('qT', h4, tS)], Wr=[ksS])
                for step in range(NBH + 2):
                    if step < NBH:
                        ks_stage1(step)
                    if step >= 2:
                        ks_stage2(step - 2)
                op('act', lambda: nc.scalar.activation(out=e_s[:, :], in_=bsS[:, 0:8 * S], func=AF.Exp, scale=QSCALE), R=[ksS], Wr=['e_s'])
                self.unpin(ksS)
                bdS, kdS = bank()
                e_s4 = e_s[:, :].rearrange("p (h m s) -> p h m s", h=4, m=2)
                mm(bdS[:, 0:4 * S].rearrange("p (h s) -> p h s", h=4), [(ones_b[:], e_s4[:, :, mc, :]) for mc in range(2)],
                   R=['e_s', 'ones_b'], Wr=[kdS])
                bpS, kpS = bank()
                for bi in range(NBH):
                    vbuf, vkey, _ = vloc(bi)
                    for h4 in range(4):
                        o0 = (h4 * NBH + bi) * 4
                        mm(bpS[:, o0:o0 + 4],
                           [(vbuf[:, mc, h4 * 128:(h4 + 1) * 128], e_s[:, ((h4 * 2 + mc) * NBH + bi) * 4:((h4 * 2 + mc) * NBH + bi) * 4 + 4])
                            for mc in range(2)],
                           R=[vkey, 'e_s'], Wr=[kpS])
                    if NBH == 8:
                        if bi < 2:
                            issue_vs(bi + 6)
                    elif bi + 2 < NBH:
                        issue_vs(bi + 2)
                op('act', lambda: nc.scalar.activation(out=rden_s[:, :], in_=bdS[:, 0:4 * S], func=AF.Ln), R=[kdS], Wr=['rden_s'])
                op('act', lambda: nc.scalar.activation(out=rden_s[:, :], in_=rden_s[:, :], func=AF.Exp, scale=-1.0), R=['rden_s'], Wr=['rden_s'])
                op('dve', lambda: nc.vector.tensor_tensor(out=yx[:, :, PH:T], in0=bpS[:, 0:4 * S].rearrange("p (h s) -> p h s", h=4),
                                                          in1=rden_s[:, :].rearrange("p (h s) -> p h s", h=4), op=ALU.mult),
                   R=[kpS, 'rden_s'], Wr=[('yx', h4, tS) for h4 in range(4)])

                self.ck(6)
                self.barrier()
                cv.reset(SCRG)
                gsig = cv.get([128, 2, 512], F32)
                gacc = cv.get([128, 2, 512], F32)
                gtmp = cv.get([128, 2, 512], F32)
                sqt = cv.get([128, 2, 512], BF16)
                lnt = cv.get([128, 3, T], F32)

                def ykeys(nb, ti):
                    nm = ('ya', 'yb', 'yc', 'yx')[nb]
                    if nb == 1:
                        if ti == len(tts) - 1:
                            return [('yb', NM)]
                        s0, n = tts[ti]
                        return [('yb', m) for m in range(s0 // 128, (s0 + n) // 128)]
                    return [(nm, j, ti) for j in range(4)]

                i_x = 0
                i_a = 0
                for dc in range(8):
                    wg, kg_ = wnext(f"G{dc}")
                    wb, kb_ = wnext(f"BR{dc}", hold=1)
                    for dd in range(1):
                        for ti, (s0, n) in enumerate(tts):
                            asl = i_a % 2
                            i_a += 1
                            for nb in range(4):
                                bg, kbg = bank()
                                bb, kbb = bank()
                                mm(bg[:, 0:n], [(wg[:, kc, nb, dd * 128:(dd + 1) * 128], xbf[:, kc, s0:s0 + n]) for kc in range(8)],
                                   R=[kg_] + allX(ti), Wr=[kbg])
                                ytok = mm(bb[:, 0:n], [(wb[:, nb, wc_, dd * 128:(dd + 1) * 128], ys[nb][:, wc_, s0:s0 + n]) for wc_ in range(4)],
                                          R=[kb_] + ykeys(nb, ti), Wr=[kbb])
                                self.lw['ydone'] = ytok
                                xs = i_x % 2
                                i_x += 1
                                op('act', lambda: nc.scalar.activation(out=gsig[:, xs, 0:n], in_=bg[:, 0:n], func=AF.Sigmoid),
                                   R=[kbg], Wr=[('gsig', xs)])
                                if nb == 0:
                                    op('dve', lambda: nc.vector.tensor_tensor(out=gacc[:, asl, 0:n], in0=bb[:, 0:n], in1=gsig[:, xs, 0:n], op=ALU.mult),
                                       R=[kbb, ('gsig', xs)], Wr=[('gacc', asl)])
                                else:
                                    op('dve', lambda: nc.vector.tensor_tensor(out=gtmp[:, xs, 0:n], in0=bb[:, 0:n], in1=gsig[:, xs, 0:n], op=ALU.mult),
                                       R=[kbb, ('gsig', xs)], Wr=[('gtmp', xs)])
                                    if nb < 3:
                                        op('dve', lambda: nc.vector.tensor_tensor(out=gacc[:, asl, 0:n], in0=gacc[:, asl, 0:n], in1=gtmp[:, xs, 0:n], op=ALU.add),
                                           R=[('gacc', asl), ('gtmp', xs)], Wr=[('gacc', asl)])
                                    else:
                                        op('dve', lambda: nc.vector.tensor_tensor(out=mixin[:, dc, s0:s0 + n], in0=gacc[:, asl, 0:n], in1=gtmp[:, xs, 0:n], op=ALU.add),
                                           R=[('gacc', asl), ('gtmp', xs)], Wr=[('mixin', dc, ti)])

                def layernorm_all(gcol, bcol, agcol, abcol, res_plain):
                    def acc_(ti):
                        s0, n = tts[ti]
                        return ln_acc(lambda c: Rr[:, c, s0:s0 + n], lambda c: Rk(c, ti), 8, onesD, n)

                    def fin_(ti, a):
                        s0, n = tts[ti]
                        ln_fin(a, s0, n, ti)

                    def app_(ti):
                        s0, n = tts[ti]
                        ln_apply(Rr, 8, [Rk(c, ti) for c in range(8)], s0, n, s0, ti)
                        for c in range(8):
                            if res_plain:
                                op('act', lambda c=c: nc.scalar.activation(out=Rr[:, c, s0:s0 + n], in_=Rr[:, c, s0:s0 + n], func=AF.Identity,
                                                                           bias=pcol(l, bcol + c), scale=pcol(l, gcol + c)),
                                   R=[('pT', l)], Wr=[Rk(c, ti)])
                            else:
                                op('dve', lambda c=c: nc.vector.tensor_scalar(out=xbf[:, c, s0:s0 + n], in0=Rr[:, c, s0:s0 + n],
                                                                              scalar1=pcol(l, gcol + c), scalar2=pcol(l, bcol + c),
                                                                              op0=ALU.mult, op1=ALU.add),
                                   R=[Rk(c, ti), ('pT', l)], Wr=[Xk(c, ti)])
                                op('act', lambda c=c: nc.scalar.activation(out=Rr[:, c, s0:s0 + n], in_=Rr[:, c, s0:s0 + n], func=AF.Identity,
                                                                           bias=pcol(l, abcol + c), scale=pcol(l, agcol + c)),
                                   R=[('pT', l)], Wr=[Rk(c, ti)])
                    nt = len(tts)
                    accs = {}
                    for step in range(nt + 2):
                        if step < nt:
                            accs[step] = acc_(step)
                        if 0 <= step - 1 < nt:
                            fin_(step - 1, accs[step - 1])
                        if 0 <= step - 2 < nt:
                            app_(step - 2)

                for hf in range(2):
                    wo, kwo = wnext(f"WO{hf}")
                    for oo in range(4):
                        oc = hf * 4 + oo
                        for ti, (s0, n) in enumerate(tts):
                            bb, kk = bank()
                            self.lw['wodone'] = mm(bb[:, 0:n], [(wo[:, kc, oo * 128:(oo + 1) * 128], mixin[:, kc, s0:s0 + n]) for kc in range(8)],
                                                   R=[kwo] + [('mixin', kc, ti) for kc in range(8)], Wr=[kk])
                            op('dve', lambda: nc.vector.tensor_tensor(out=Rr[:, oc, s0:s0 + n], in0=Rr[:, oc, s0:s0 + n], in1=bb[:, 0:n], op=ALU.add),
                               R=[kk, Rk(oc, ti)], Wr=[Rk(oc, ti)])
                layernorm_all(L1G, L1B, AG1, AB1, False)

                self.ck(7)
                _save = cv.off
                cv.reset(0)
                a_sb = cv.get([128, 16, T], BF16)
                cv.reset(_save)
                a1 = gsig
                i_x = 0
                if nxt is not None:
                    layer_prep_dma(*nxt)
                for fh in range(2):
                    if fh == 1 and nxt is not None:
                        layer_prep_compute(*nxt)
                    for gp in range(2):
                        paired = (fh == 0 and gp == 0)
                        if paired:
                            wu0, ku0 = wnext(f"UP{fh * 4 + gp * 2}")
                            wu1, ku1 = wnext(f"UP{fh * 4 + gp * 2 + 1}", hold=1)
                            sched = [(ti, s0, n, g, wu, ku) for ti, (s0, n) in enumerate(tts)
                                     for (g, wu, ku) in ((gp * 2, wu0, ku0), (gp * 2 + 1, wu1, ku1))]
                        else:
                            sched = None
                        for gsel in ([None] if paired else [gp * 2, gp * 2 + 1]):
                            if not paired:
                                wu_, ku_ = wnext(f"UP{fh * 4 + gsel}")
                                sched = [(ti, s0, n, gsel, wu_, ku_) for ti, (s0, n) in enumerate(tts)]
                            for (ti, s0, n, g, wu, ku) in sched:
                              for _once in (0,):
                                for ff in range(4):
                                    fcl = g * 4 + ff
                                    fc = fh * 16 + fcl
                                    bb, kk = bank()
                                    mm(bb[:, 0:n], [(wu[:, kc, ff * 128:(ff + 1) * 128], xbf[:, kc, s0:s0 + n]) for kc in range(8)],
                                       R=[ku] + allX(ti), Wr=[kk])
                                    xs = i_x % 2
                                    i_x += 1
                                    op('act', lambda: nc.scalar.activation(out=a1[:, xs, 0:n], in_=bb[:, 0:n], func=AF.Relu, bias=pcol(l, BUP + fc)),
                                       R=[kk, ('pT', l)], Wr=[('gsig', xs)])
                                    op('dve', lambda: nc.vector.tensor_tensor(out=a_sb[:, fcl, s0:s0 + n], in0=a1[:, xs, 0:n], in1=a1[:, xs, 0:n], op=ALU.mult),
                                       R=[('gsig', xs), 'ydone'], Wr=[('a', fcl, ti)])
                    for dcp in range(4):
                        wd, kd_ = wnext(f"DN{fh}_{dcp}")
                        for dd in range(2):
                            dc = dcp * 2 + dd
                            for ti, (s0, n) in enumerate(tts):
                                bb, kk = bank()
                                self.lw['adone'] = mm(bb[:, 0:n], [(wd[:, fcl, dd * 128:(dd + 1) * 128], a_sb[:, fcl, s0:s0 + n]) for fcl in range(16)],
                                                      R=[kd_] + [('a', fcl, ti) for fcl in range(16)], Wr=[kk])
                                op('dve', lambda: nc.vector.tensor_tensor(out=Rr[:, dc, s0:s0 + n], in0=Rr[:, dc, s0:s0 + n], in1=bb[:, 0:n], op=ALU.add),
                                   R=[kk, Rk(dc, ti)], Wr=[Rk(dc, ti)])
                layernorm_all(L2G, L2B, AG2, AB2, last)

                if last:
                    self.barrier()
                    cv.reset(0)
                    yout = cv.get([128, 2, D], F32)
                    for m in range(NM + 1):
                        smp = (m == NM)
                        rows = S if smp else 128
                        c0 = PH if smp else m * 128
                        ti = tt_of_tile(m)
                        sl = m % 2
                        for kg in range(2):
                            pb, pk = bank()
                            for i in range(4):
                                kc = kg * 4 + i
                                op('pe', lambda i=i, kc=kc: nc.tensor.transpose(out=pb[0:rows, i * 128:(i + 1) * 128],
                                                                                in_=Rr[:, kc, c0:c0 + rows], identity=ident_f[:]),
                                   R=[Rk(kc, ti), 'ident_f'], Wr=[pk])
                            if kg == 0:
                                op('act', lambda: nc.scalar.copy(out=yout[0:rows, sl, 0:512], in_=pb[0:rows, :]), R=[pk], Wr=[('yout', sl, 0)])
                            else:
                                op('dve', lambda: nc.vector.tensor_copy(out=yout[0:rows, sl, 512:1024], in_=pb[0:rows, :]), R=[pk], Wr=[('yout', sl, 1)])
                        dst = y_sample[h * S:(h + 1) * S, :] if smp else y_prompt[h * PH + m * 128:h * PH + (m + 1) * 128, :]
                        dma('sp', dst, yout[0:rows, sl, :], dsem_y[sl], R=[('yout', sl, 0), ('yout', sl, 1)])


_CACHE = {}


def _get_nc(SEQ, NB, DEPTH):
    key = (SEQ, NB, DEPTH)
    if key not in _CACHE:
        _CACHE[key] = Builder(SEQ, NB, DEPTH).build()
    return _CACHE[key]


def make_in_map(inputs, c, SEQ, NB, DEPTH):
    f = lambda a: np.ascontiguousarray(np.asarray(a, dtype=np.float32))
    g = inputs
    m = {
        "x_prompt": f(g["x_prompt"][c]),
        "x_sample": f(g["x_sample"][c * NB:(c + 1) * NB]).reshape(NB * 4, D),
        "mem_prompt": f(g["mem_prompt"][c]),
        "state_conv_a": f(g["state_conv_a"][:, c * NB:(c + 1) * NB]).reshape(DEPTH, NB * 2, W),
        "state_conv_c": f(g["state_conv_c"][:, c * NB:(c + 1) * NB]),
        "cache_mem_k": f(g["cache_mem_k"][:, c * NB:(c + 1) * NB]).reshape(DEPTH, NB, NMEM, W),
        "cache_mem_v": f(g["cache_mem_v"][:, c * NB:(c + 1) * NB]).reshape(DEPTH, NB, NMEM, W),
        "w_in": f(g["w_in"]),
        "w_conv_a": f(g["w_conv_a"]).reshape(DEPTH, 12, 128),
        "ln_v_g": f(g["ln_v_g"]), "ln_v_b": f(g["ln_v_b"]),
        "w_s": f(g["w_s"]), "b_s": f(g["b_s"]),
        "w_conv_c": f(g["w_conv_c"]).reshape(DEPTH, 124, 128),
        "b_conv_c": f(g["b_conv_c"]).reshape(DEPTH, 4, 128),
        "ln_c_g": f(g["ln_c_g"]).reshape(DEPTH, 4, 128),
        "ln_c_b": f(g["ln_c_b"]).reshape(DEPTH, 4, 128),
        "w_mem_kv": f(g["w_mem_kv"]), "w_out_br": f(g["w_out_br"]), "w_o": f(g["w_o"]),
        "ln1_g": f(g["ln1_g"]).reshape(DEPTH, 8, 128), "ln1_b": f(g["ln1_b"]).reshape(DEPTH, 8, 128),
        "w_up": f(g["w_up"]), "b_up": f(g["b_up"]).reshape(DEPTH, 32, 128),
        "w_down": f(g["w_down"]), "b_down": f(g["b_down"]).reshape(DEPTH, 8, 128),
        "ln2_g": f(g["ln2_g"]).reshape(DEPTH, 8, 128), "ln2_b": f(g["ln2_b"]).reshape(DEPTH, 8, 128),
    }
    return m


def run(inputs, n_cores, SEQ, NB, DEPTH, trace=False):
    nc = _get_nc(SEQ, NB, DEPTH)
    in_maps = [make_in_map(inputs, c, SEQ, NB, DEPTH) for c in range(n_cores)]
    res = run_bass_kernel_spmd(nc, in_maps, core_ids=list(range(n_cores)), trace=trace)
    r = res.results
    B = n_cores
    yp = np.stack([r[c]["y_prompt"] for c in range(B)], 0)
    ysm = np.concatenate([r[c]["y_sample"].reshape(NB, 4, D) for c in range(B)], 0)
    cap = np.stack([r[c]["new_conv_a_prompt"] for c in range(B)], 1)
    ccp = np.stack([r[c]["new_conv_c_prompt"] for c in range(B)], 1)
    mk = np.stack([r[c]["mem_k_prompt"].reshape(DEPTH, NMEM, 4, 128) for c in range(B)], 1)
    mv = np.stack([r[c]["mem_v_prompt"].reshape(DEPTH, NMEM, 4, 128) for c in range(B)], 1)
    cas = np.concatenate([r[c]["new_conv_a_sample"].reshape(DEPTH, NB, 2, W) for c in range(B)], 1)
    ccs = np.concatenate([r[c]["new_conv_c_sample"] for c in range(B)], 1)
    vs = np.concatenate([r[c]["chunk_v_sample"].reshape(DEPTH, NB, 4, W) for c in range(B)], 1)
    outs = tuple(np.ascontiguousarray(o, dtype=np.float32) for o in (yp, ysm, cap, ccp, mk, mv, cas, ccs, vs))
    return outs, res


def kernel(**inputs):
    outs, _ = run(inputs, 8, 2048, 16, 4)
    return outs
```

```python
import numpy as np
import concourse.bass as bass
import concourse.mybir as mybir
from concourse.bass_utils import run_bass_kernel_spmd

F32 = mybir.dt.float32
BF16 = mybir.dt.bfloat16
AF = mybir.ActivationFunctionType
ALU = mybir.AluOpType

D = 1024
W = 512
NMEM = 256
DFF = 4096
INC = 8192
ALPHA = float((2 * 4) ** 0.25)
EPS = 1e-5
QSCALE = float(128 ** -0.5)

WC, BC, WA, LCG, LCB, L1G, L1B, BUP, BDN, L2G, L2B, AG1, AB1, AG2, AB2 = (
    0, 124, 128, 140, 144, 148, 156, 164, 196, 204, 212, 220, 228, 236, 244)


import os


class StopBuild(Exception):
    pass


class DSem:
    def __init__(self, nc, name):
        self.sem = nc.alloc_semaphore(name)
        self.cnt = 0


class Builder:
    def __init__(self, SEQ, NB, DEPTH, NSLOT=4):
        self.SEQ, self.NB, self.DEPTH = SEQ, NB, DEPTH
        self.PH = SEQ // 2
        self.NBH = NB // 2
        self.S = self.NBH * 4
        self.T = self.PH + self.S
        self.NM = self.PH // 128
        self.tts = [(s, min(512, self.PH - s)) for s in range(0, self.PH, 512)] + [(self.PH, self.S)]
        self.NSLOT = NSLOT
        self.nc = bass.Bass("TRN2", target_bir_lowering=False)
        nc = self.nc
        self.E = {'pe': nc.tensor, 'act': nc.scalar, 'dve': nc.vector, 'pool': nc.gpsimd, 'sp': nc.sync}
        self.sem = {}
        self.cnt = {}
        self.waited = {e: {} for e in self.E}
        self.lw = {}
        self.rd = {}
        self.phase_id = 0
        self.store_sems = []
        self.arena_load_sems = []
        self.bank_i = 0
        self.pinned = set()

    def new_phase(self):
        for e in ('pe', 'act', 'dve', 'pool'):
            self.sem[e] = self.nc.alloc_semaphore(f"s_{e}_{self.phase_id}")
            self.cnt[e] = 0
        self.phase_id += 1

    def _waits(self, e, R, Wr):
        toks = {}
        for k in R:
            t = self.lw.get(k)
            if t is not None:
                toks[(t[0].num, t[1])] = t
        for k in Wr:
            t = self.lw.get(k)
            if t is not None:
                toks[(t[0].num, t[1])] = t
            for t in self.rd.get(k, {}).values():
                toks[(t[0].num, t[1])] = t
        eng = self.E[e]
        wd = self.waited[e]
        for t in toks.values():
            sem, val, src = t
            if src == 'pe' and e == 'pe':
                continue
            if wd.get(sem.num, 0) >= val:
                continue
            eng.wait_ge(sem, val)
            wd[sem.num] = val

    def _record(self, tok, R, Wr):
        for k in Wr:
            self.lw[k] = tok
            self.rd[k] = {}
        for k in R:
            d = self.rd.setdefault(k, {})
            old = d.get(tok[0].num)
            if old is None or old[1] < tok[1]:
                d[tok[0].num] = tok

    def op(self, e, fn, R=(), Wr=()):
        psr = [k for k in R if isinstance(k, tuple) and k[0] == 'ps']
        if psr:
            R = [k for k in R if not (isinstance(k, tuple) and k[0] == 'ps')]
            Wr = list(Wr) + psr
        self._waits(e, R, Wr)
        ins = fn()
        self.cnt[e] += 1
        ins.then_inc(self.sem[e], 1)
        tok = (self.sem[e], self.cnt[e], e)
        self._record(tok, R, Wr)
        return tok

    def dma(self, e, out, in_, dsem, R=(), Wr=(), group=None):
        self._waits(e, R, Wr)
        ins = self.E[e].dma_start(out=out, in_=in_)
        dsem.cnt += 16
        ins.then_inc(dsem.sem, 16)
        if group is not None:
            group.append((dsem, R, Wr))
            return None
        tok = (dsem.sem, dsem.cnt, 'dma')
        self._record(tok, R, Wr)
        return tok

    def commit(self, group):
        for (dsem, R, Wr) in group:
            self._record((dsem.sem, dsem.cnt, 'dma'), R, Wr)
        del group[:]

    def barrier(self):
        items = [(self.sem[f], self.cnt[f], f) for f in ('pe', 'act', 'dve', 'pool') if self.cnt[f] > 0]
        items += [(ds.sem, ds.cnt, 'dma') for ds in self.store_sems + self.arena_load_sems if ds.cnt > 0]
        for e in ('pe', 'act', 'dve', 'pool', 'sp'):
            for (sem, val, f) in items:
                if f == e and e == 'pe':
                    continue
                if self.waited[e].get(sem.num, 0) >= val:
                    continue
                self.E[e].wait_ge(sem, val)
                self.waited[e][sem.num] = val

    def snapshot(self):
        items = [(self.sem[f], self.cnt[f], f) for f in ('pe', 'act', 'dve', 'pool') if self.cnt[f] > 0]
        items += [(ds.sem, ds.cnt, 'dma') for ds in self.store_sems + self.arena_load_sems if ds.cnt > 0]
        return items

    def apply_snapshot(self, items):
        for e in ('pe', 'act', 'dve', 'pool', 'sp'):
            for (sem, val, f) in items:
                if self.waited[e].get(sem.num, 0) >= val:
                    continue
                self.E[e].wait_ge(sem, val)
                self.waited[e][sem.num] = val

    def bank(self, pin=False):
        i = self.bank_i
        while i in self.pinned:
            i = (i + 1) % 8
        self.bank_i = (i + 1) % 8
        if pin:
            self.pinned.add(i)
        return self.ps[i], ('ps', i)

    def unpin(self, key):
        self.pinned.discard(key[1])

    def mm(self, out, pairs, R, Wr):
        nc = self.nc

        def fn():
            n = len(pairs)
            ins = None
            for i, (l, r) in enumerate(pairs):
                ins = nc.tensor.matmul(out, l, r, start=(i == 0), stop=(i == n - 1))
            return ins
        return self.op('pe', fn, R, Wr)

    def build(self):
        try:
            self._build()
        except StopBuild:
            pass
        nc = self.nc
        for ds in self.store_sems:
            if ds.cnt > 0:
                nc.sync.wait_ge(ds.sem, ds.cnt)
        return nc

    def ck(self, stage):
        if int(os.environ.get('KSTOP', '99')) == stage:
            raise StopBuild()

    def _build(self):
        nc = self.nc
        SEQ, NB, DEPTH, PH, NBH, S, T, NM = self.SEQ, self.NB, self.DEPTH, self.PH, self.NBH, self.S, self.T, self.NM
        op, dma, mm, bank = self.op, self.dma, self.mm, self.bank
        tts = self.tts
        NT = 2 + 2 * NBH + 30 + 4 * NBH
        TA_P, TA_S, TC_P, TC_S = 0, 2, 2 + 2 * NBH, 32 + 2 * NBH
        self.new_phase()

        def din(name, shape):
            return nc.dram_tensor(name, list(shape), F32, kind="ExternalInput").ap()

        def dout(name, shape):
            return nc.dram_tensor(name, list(shape), F32, kind="ExternalOutput").ap()

        x_prompt = din("x_prompt", [SEQ, D])
        x_sample = din("x_sample", [NB * 4, D])
        mem_prompt = din("mem_prompt", [NMEM, D])
        state_a = din("state_conv_a", [DEPTH, NB * 2, W])
        state_c = din("state_conv_c", [DEPTH, NB, 30, W])
        cache_k = din("cache_mem_k", [DEPTH, NB, NMEM, W])
        cache_v = din("cache_mem_v", [DEPTH, NB, NMEM, W])
        w_in = din("w_in", [DEPTH, D, INC])
        w_conv_a = din("w_conv_a", [DEPTH, 12, 128])
        ln_v_g = din("ln_v_g", [DEPTH, W])
        ln_v_b = din("ln_v_b", [DEPTH, W])
        w_s = din("w_s", [DEPTH, 4, 128, 128])
        b_s = din("b_s", [DEPTH, 4, 128])
        w_conv_c = din("w_conv_c", [DEPTH, 124, 128])
        b_conv_c = din("b_conv_c", [DEPTH, 4, 128])
        ln_c_g = din("ln_c_g", [DEPTH, 4, 128])
        ln_c_b = din("ln_c_b", [DEPTH, 4, 128])
        w_mem_kv = din("w_mem_kv", [DEPTH, D, 2 * W])
        w_out_br = din("w_out_br", [DEPTH, 4, W, D])
        w_o = din("w_o", [DEPTH, D, D])
        ln1_g = din("ln1_g", [DEPTH, 8, 128])
        ln1_b = din("ln1_b", [DEPTH, 8, 128])
        w_up = din("w_up", [DEPTH, D, DFF])
        b_up = din("b_up", [DEPTH, 32, 128])
        w_down = din("w_down", [DEPTH, DFF, D])
        b_down = din("b_down", [DEPTH, 8, 128])
        ln2_g = din("ln2_g", [DEPTH, 8, 128])
        ln2_b = din("ln2_b", [DEPTH, 8, 128])

        y_prompt = dout("y_prompt", [SEQ, D])
        y_sample = dout("y_sample", [NB * 4, D])
        o_ca_p = dout("new_conv_a_prompt", [DEPTH, 2, W])
        o_cc_p = dout("new_conv_c_prompt", [DEPTH, 30, W])
        o_mk = dout("mem_k_prompt", [DEPTH, NMEM, W])
        o_mv = dout("mem_v_prompt", [DEPTH, NMEM, W])
        o_ca_s = dout("new_conv_a_sample", [DEPTH, NB * 2, W])
        o_cc_s = dout("new_conv_c_sample", [DEPTH, NB, 30, W])
        o_v_s = dout("chunk_v_sample", [DEPTH, NB * 4, W])

        def sb(name, shape, dt):
            return nc.alloc_sbuf_tensor(name, list(shape), dt)

        ident_f = sb("ident_f", [128, 128], F32)
        ident_b = sb("ident_b", [128, 128], BF16)
        onesW = sb("onesW", [128, 128], F32)
        onesD = sb("onesD", [128, 128], F32)
        ones_b = sb("ones_b", [128, 128], BF16)
        onesWb = sb("onesWb", [128, 128], BF16)
        onesDb = sb("onesDb", [128, 128], BF16)
        ones_row = sb("ones_row", [1, 128], F32)
        neghalf = sb("neghalf", [128, 512], F32)
        pT = sb("pT", [128, DEPTH, 256], F32)
        memT = sb("memT", [128, 8, NMEM], BF16)
        Rr = sb("R", [128, 8, T], F32)
        xbf = sb("xbf", [128, 8, T], BF16)
        wsT = sb("wsT", [128, 4, 128], BF16)
        wsTf = sb("wsTf", [128, 4, 128], F32)
        wblk = sb("wblk", [32, 4, 32], BF16)
        wblkf = sb("wblkf", [32, 4, 32], F32)
        bs_row = sb("bs_row", [1, 4, 128], F32)
        bsS_row = sb("bsS_row", [1, 4, NBH, 4], F32)
        lnv_g = sb("lnv_g", [128, W], F32)
        lnv_b = sb("lnv_b", [128, W], F32)
        pa_hist = sb("pa_hist", [128, DEPTH, 4, 2], F32)
        glu_hist = sb("glu_hist", [128, DEPTH, 4, 30], F32)
        p_s = sb("p_s", [128, 4, NBH, 6], F32)
        glu_s = sb("glu_s", [128, 4, NBH, 34], F32)
        tails = sb("tails", [128, 4, NT], F32)
        tailsT = sb("tailsT", [128, 512], F32)
        st_stage = sb("st_stage", [128, 2, 512], F32)
        ring = [sb(f"ring{i}", [128, 4096], BF16) for i in range(self.NSLOT)]
        ring_sem = [DSem(nc, f"ringsem{i}") for i in range(self.NSLOT)]
        self.ps = [nc.alloc_psum_tensor(f"ps{i}", [128, 512], F32) for i in range(8)]

        rem = nc.sbuf_bytes_remaining
        ARENA_B = (rem - 1024) // 64 * 64
        arena = sb("arena", [128, ARENA_B // 4], F32)

        class Carver:
            def __init__(s):
                s.off = 0

            def reset(s, off=0):
                s.off = off

            def get(s, shape, dt):
                esz = 4 if dt == F32 else 2
                n = int(np.prod(shape[1:]))
                nbytes = (n * esz + 63) // 64 * 64
                assert s.off + nbytes <= ARENA_B, (s.off, nbytes, ARENA_B)
                v = arena[0:shape[0], s.off // 4:(s.off + nbytes) // 4]
                if dt != F32:
                    v = v.bitcast(dt)
                v = v[:, 0:n]
                if len(shape) == 3:
                    v = v.rearrange("p (a b) -> p a b", a=shape[1])
                elif len(shape) == 4:
                    v = v.rearrange("p (a b c) -> p a b c", a=shape[1], b=shape[2])
                s.off += nbytes
                return v

        cv = Carver()
        ya = cv.get([128, 4, T], BF16)
        yb = cv.get([128, 4, T], BF16)
        yc = cv.get([128, 4, T], BF16)
        yx = cv.get([128, 4, T], BF16)
        ys = [ya, yb, yc, yx]
        SCR0 = cv.off
        mixin = cv.get([128, 8, T], BF16)
        SCRG = cv.off

        dsem_misc = DSem(nc, "d_misc")
        dsem_x = [DSem(nc, "d_x0"), DSem(nc, "d_x1")]
        dsem_y = [DSem(nc, "d_y0"), DSem(nc, "d_y1")]
        dsem_tail = DSem(nc, "d_tail")
        dsem_kvo = [DSem(nc, "d_kvo0"), DSem(nc, "d_kvo1")]
        dsem_vo = DSem(nc, "d_vo")
        dsem_st = DSem(nc, "d_st")
        dsem_lay = DSem(nc, "d_lay")
        dsem_ks = [DSem(nc, f"d_ks{i}") for i in range(4)]
        dsem_vs = [DSem(nc, "d_vs0"), DSem(nc, "d_vs1")]
        dsem_d2d = DSem(nc, "d_d2d")
        self.store_sems = [dsem_y[0], dsem_y[1], dsem_tail, dsem_kvo[0], dsem_kvo[1], dsem_vo, dsem_d2d]
        self.arena_load_sems = [dsem_misc, dsem_x[0], dsem_x[1]] + dsem_ks + dsem_vs

        op('pool', lambda: nc.gpsimd.memset(ident_f[:], 0.0), Wr=['ident_f'])
        op('pool', lambda: nc.gpsimd.affine_select(out=ident_f[:], in_=ident_f[:], compare_op=ALU.not_equal,
                                                   fill=1.0, base=0, pattern=[[-1, 128]], channel_multiplier=1),
           R=['ident_f'], Wr=['ident_f'])
        op('dve', lambda: nc.vector.tensor_copy(out=ident_b[:], in_=ident_f[:]), R=['ident_f'], Wr=['ident_b'])
        op('dve', lambda: nc.vector.memset(onesW[:], 1.0 / W), Wr=['onesW'])
        op('dve', lambda: nc.vector.memset(onesD[:], 1.0 / D), Wr=['onesD'])
        op('dve', lambda: nc.vector.memset(ones_b[:], 1.0), Wr=['ones_b'])
        op('dve', lambda: nc.vector.memset(onesWb[:], 1.0 / W), Wr=['onesWb'])
        op('dve', lambda: nc.vector.memset(onesDb[:], 1.0 / D), Wr=['onesDb'])
        op('dve', lambda: nc.vector.memset(ones_row[:], 1.0), Wr=['ones_row'])
        op('dve', lambda: nc.vector.memset(neghalf[:], -0.5), Wr=['neghalf'])
        op('dve', lambda: nc.vector.memset(tails[:], 0.0), Wr=['tails'])
        op('dve', lambda: nc.vector.memset(wblkf[:], 0.0), Wr=['wblkf'])

        cv.reset(SCR0)
        pstage = cv.get([128, DEPTH, 2, 128], F32)
        mem_sb = cv.get([128, 2, D], F32)
        op('dve', lambda: nc.vector.memset(pstage[:], 0.0), Wr=['pstage'])
        pskeys = {}
        grp = []
        for l in range(DEPTH):
            rows0 = [(w_conv_c[l], 0, 124), (b_conv_c[l], 124, 4)]
            rows1 = [(w_conv_a[l], 0, 12), (ln_c_g[l], 12, 4), (ln_c_b[l], 16, 4), (ln1_g[l], 20, 8),
                     (ln1_b[l], 28, 8), (b_up[l], 36, 32), (b_down[l], 68, 8), (ln2_g[l], 76, 8), (ln2_b[l], 84, 8)]
            for slot, rows in ((0, rows0), (1, rows1)):
                for (src, r0, n) in rows:
                    dma('sp', pstage[r0:r0 + n, l, slot, :], src, dsem_misc, R=['pstage'], Wr=[('pstage', l, slot, r0)], group=grp)
                    pskeys.setdefault((l, slot), []).append(('pstage', l, slot, r0))
        dma('sp', mem_sb[:], mem_prompt.rearrange("(mc p) d -> p mc d", p=128), dsem_misc, Wr=['mem_sb'], group=grp)
        self.commit(grp)
        for l in range(DEPTH):
            pb, pk = bank()
            for slot in range(2):
                op('pe', lambda slot=slot: nc.tensor.transpose(out=pb[:, slot * 128:(slot + 1) * 128],
                                                               in_=pstage[:, l, slot, :], identity=ident_f[:]),
                   R=['pstage', 'ident_f'] + pskeys[(l, slot)], Wr=[pk])
            op('dve', lambda: nc.vector.tensor_copy(out=pT[:, l, :], in_=pb[:, 0:256]), R=[pk], Wr=[('pT', l)])
            op('dve', lambda: nc.vector.tensor_scalar(out=pT[:, l, AG1:AG1 + 8], in0=pT[:, l, L1G:L1G + 8],
                                                      scalar1=ALPHA, scalar2=None, op0=ALU.mult),
               R=[('pT', l)], Wr=[('pT', l)])
            op('dve', lambda: nc.vector.scalar_tensor_tensor(out=pT[:, l, AB1:AB1 + 8], in0=pT[:, l, L1B:L1B + 8],
                                                             scalar=ALPHA, in1=pT[:, l, BDN:BDN + 8],
                                                             op0=ALU.mult, op1=ALU.add),
               R=[('pT', l)], Wr=[('pT', l)])
            op('dve', lambda: nc.vector.tensor_scalar(out=pT[:, l, AG2:AG2 + 8], in0=pT[:, l, L2G:L2G + 8],
                                                      scalar1=ALPHA, scalar2=None, op0=ALU.mult),
               R=[('pT', l)], Wr=[('pT', l)])
            op('dve', lambda: nc.vector.tensor_scalar(out=pT[:, l, AB2:AB2 + 8], in0=pT[:, l, L2B:L2B + 8],
                                                      scalar1=ALPHA, scalar2=None, op0=ALU.mult),
               R=[('pT', l)], Wr=[('pT', l)])

        def pcol(l, c):
            return pT[:, l, c:c + 1]

        for mc in range(2):
            for kg in range(2):
                pb, pk = bank()
                for i in range(4):
                    kc = kg * 4 + i
                    op('pe', lambda i=i, kc=kc: nc.tensor.transpose(out=pb[:, i * 128:(i + 1) * 128],
                                                                    in_=mem_sb[:, mc, kc * 128:(kc + 1) * 128],
                                                                    identity=ident_f[:]),
                       R=['mem_sb', 'ident_f'], Wr=[pk])
                op('act', lambda: nc.scalar.copy(out=memT[:, kg * 4:kg * 4 + 4, mc * 128:(mc + 1) * 128],
                                                 in_=pb[:, :].rearrange("p (a b) -> p a b", a=4)),
                   R=[pk], Wr=['memT'])

        def wplan(l):
            P = []
            wi = w_in[l]
            for j in range(4):
                src = wi[:, 0:1536].rearrange("(kc p) (s q c) -> p kc s q c", p=128, s=3, q=4)[:, :, :, j, :]
                P.append((f"A{j}", [128, 8, 3, 128], src))
            P.append(("U", [128, 8, 512], wi[:, 1536:2048].rearrange("(kc p) n -> p kc n", p=128)))
            P.append(("V", [128, 8, 512], wi[:, 2048:2560].rearrange("(kc p) n -> p kc n", p=128)))
            for jt in range(2):
                src = wi[:, 2560:3584].rearrange("(kc p) (s q c) -> p kc s q c", p=128, s=2, q=2)[:, :, :, jt, :]
                P.append((f"C{jt}", [128, 8, 2, 256], src))
            P.append(("KVK", [128, 8, 512], w_mem_kv[l][:, 0:512].rearrange("(kc p) n -> p kc n", p=128)))
            P.append(("KVV", [128, 8, 512], w_mem_kv[l][:, 512:1024].rearrange("(kc p) n -> p kc n", p=128)))
            P.append(("Q", [128, 8, 512], wi[:, 3584:4096].rearrange("(kc p) n -> p kc n", p=128)))
            for dc in range(8):
                src = wi[:, 4096:8192].rearrange("(kc p) (n q c) -> p kc n q c", p=128, n=4, q=8)[:, :, :, dc, :]
                P.append((f"G{dc}", [128, 8, 4, 128], src))
                src = w_out_br[l].rearrange("n (wc p) (q c) -> p n wc q c", p=128, q=8)[:, :, :, dc, :]
                P.append((f"BR{dc}", [128, 4, 4, 128], src))
            for hf in range(2):
                P.append((f"WO{hf}", [128, 8, 512], w_o[l][:, hf * 512:(hf + 1) * 512].rearrange("(kc p) n -> p kc n", p=128)))
            for fh in range(2):
                for g in range(4):
                    gg = fh * 4 + g
                    P.append((f"UP{gg}", [128, 8, 512], w_up[l][:, gg * 512:(gg + 1) * 512].rearrange("(kc p) n -> p kc n", p=128)))
                for dcp in range(4):
                    src = w_down[l][fh * 2048:(fh + 1) * 2048, :].rearrange("(fc p) (q c) -> p fc q c", p=128, q=4)[:, :, dcp, :]
                    P.append((f"DN{fh}_{dcp}", [128, 16, 256], src))
            return P

        plan = []
        for h in range(2):
            for l in range(DEPTH):
                plan += [(f"h{h}l{l}{n}", shp, src) for (n, shp, src) in wplan(l)]
        rstate = {'issued': 0, 'next': 0}

        def ring_view(i, shp):
            slot = i % self.NSLOT
            n = int(np.prod(shp[1:]))
            v = ring[slot][:, 0:n]
            if len(shp) == 3:
                v = v.rearrange("p (a b) -> p a b", a=shp[1])
            else:
                v = v.rearrange("p (a b c) -> p a b c", a=shp[1], b=shp[2])
            return v

        def ring_issue_upto(i):
            while rstate['issued'] <= min(i, len(plan) - 1):
                j = rstate['issued']
                name, shp, src = plan[j]
                slot = j % self.NSLOT
                rv = ring_view(j, shp)
                if len(shp) == 4 and not name[4:].startswith('BR'):
                    for q in range(shp[2]):
                        dma('pool', rv[:, :, q, :], src[:, :, q, :], ring_sem[slot], R=[], Wr=[('ring', slot)], group=grp)
                    self.commit(grp)
                else:
                    dma('pool', rv, src, ring_sem[slot], R=[], Wr=[('ring', slot)])
                rstate['issued'] += 1

        def wnext(name, hold=0):
            i = rstate['next']
            assert plan[i][0].endswith(name), (plan[i][0], name)
            ring_issue_upto(i + self.NSLOT - 1 - hold)
            rstate['next'] += 1
            return ring_view(i, plan[i][1]), ('ring', i % self.NSLOT)

        ring_issue_upto(self.NSLOT - 2)

        self.ck(0)
        for h in range(2):
            self.barrier()
            self.ck(10)
            cv.reset(SCR0)
            xin = cv.get([128, 2, D], F32)
            for m in range(NM + 1):
                sl = m % 2
                rows = 128 if m < NM else S
                c0 = m * 128 if m < NM else PH
                src = x_prompt[h * PH + m * 128:h * PH + m * 128 + 128, :] if m < NM else x_sample[h * S:(h + 1) * S, :]
                dma('sp', xin[0:rows, sl, :], src, dsem_x[sl], Wr=[('xin', sl)])
                self.ck(11)
                for kg in range(2):
                    pb, pk = bank()
                    for i in range(4):
                        kc = kg * 4 + i
                        op('pe', lambda i=i, kc=kc: nc.tensor.transpose(out=pb[:, i * 128:i * 128 + rows],
                                                                        in_=xin[0:rows, sl, kc * 128:(kc + 1) * 128],
                                                                        identity=ident_f[0:rows, 0:rows]),
                           R=[('xin', sl), 'ident_f'], Wr=[pk])
                    self.ck(12)
                    pv = pb[:, :].rearrange("p (a b) -> p a b", a=4)[:, :, 0:rows]
                    keys = [('R', kg * 4 + i, c0 // 512 if m < NM else len(tts) - 1) for i in range(4)]
                    xkeys = [('xbf', kg * 4 + i, c0 // 512 if m < NM else len(tts) - 1) for i in range(4)]
                    op('act', lambda: nc.scalar.mul(out=Rr[:, kg * 4:kg * 4 + 4, c0:c0 + rows], in_=pv, mul=ALPHA),
                       R=[pk], Wr=keys)
                    self.ck(13)
                    op('dve', lambda: nc.vector.tensor_copy(out=xbf[:, kg * 4:kg * 4 + 4, c0:c0 + rows], in_=pv),
                       R=[pk], Wr=xkeys)
                    self.ck(14)
                self.ck(15)
                if m == NM - 1:
                    self.ck(16)

            self.ck(1)

            def Rk(c, ti):
                return ('R', c, ti)

            def Xk(c, ti):
                return ('xbf', c, ti)

            def allX(ti):
                return [Xk(c, ti) for c in range(8)]

            def tt_of_tile(m):
                return (m * 128) // 512 if m < NM else len(tts) - 1

            for l in range(DEPTH):
                snap = None
                if l == 0:
                    self.barrier()
                else:
                    snap = self.snapshot()
                self.new_phase()
                last = (l == DEPTH - 1)
                b0 = h * NBH

                def layer_prep_dma(h_, l_):
                    l = l_
                    b0 = h_ * NBH
                    nb2 = NBH // 2
                    dma('sp', wsTf[:], w_s[l].rearrange("g t s -> t g s"), dsem_lay, Wr=['wsTf'], group=grp)
                    dma('sp', bs_row[:], b_s[l:l + 1], dsem_lay, Wr=['bs_row'], group=grp)
                    dma('sp', bsS_row[:], b_s[l:l + 1, :, 0:4].unsqueeze(2).broadcast_to([1, 4, NBH, 4]), dsem_lay, Wr=['bsS_row'], group=grp)
                    dma('sp', lnv_g[:], ln_v_g[l].partition_broadcast(128), dsem_lay, Wr=['lnv_g'], group=grp)
                    dma('sp', lnv_b[:], ln_v_b[l].partition_broadcast(128), dsem_lay, Wr=['lnv_b'], group=grp)
                    dma('sp', tailsT[0:2 * NBH, :], state_a[l, 2 * b0:2 * b0 + 2 * NBH, :], dsem_lay, R=['tailsT'], Wr=[('st_stage', 0), 'tailsT'], group=grp)
                    for q in range(2):
                        dma('sp', st_stage[0:nb2 * 30, q, :],
                            state_c[l, b0 + q * nb2:b0 + (q + 1) * nb2].rearrange("b k c -> (b k) c"), dsem_lay, Wr=[('st_stage', 1 + q)], group=grp)
                    self.commit(grp)
                    dma('sp', o_cc_s[l, b0:b0 + NBH, 0:26, :], state_c[l, b0:b0 + NBH, 4:30, :], dsem_d2d)


                def layer_prep_compute(h_, l_):
                    l = l_
                    b0 = h_ * NBH
                    nb2 = NBH // 2
                    pb, pk = bank()
                    for g in range(4):
                        op('pe', lambda g=g: nc.tensor.transpose(out=pb[:, g * 128:(g + 1) * 128], in_=wsTf[:, g, :],
                                                                 identity=ident_f[:]),
                           R=['wsTf', 'ident_f'], Wr=[pk])
                    op('dve', lambda: nc.vector.tensor_copy(out=wsTf[:].rearrange("p a b -> p (a b)"), in_=pb[:, :]),
                       R=[pk], Wr=['wsTf'])
                    for bi in range(NBH):
                        dma('sp', wblkf[4 * bi:4 * bi + 4, :, 4 * bi:4 * bi + 4], wsTf[0:4, :, 0:4], dsem_st,
                            R=['wsTf', 'wblkf'], Wr=[('wblkf', bi)], group=grp)
                    self.commit(grp)
                    for g in range(4):
                        op('pool', lambda g=g: nc.gpsimd.affine_select(out=wsTf[:, g, :], in_=wsTf[:, g, :],
                                                                       compare_op=ALU.is_ge, fill=0.0, base=0,
                                                                       pattern=[[1, 128]], channel_multiplier=-1),
                           R=['wsTf', 'wblkf'] + [('wblkf', bi) for bi in range(NBH)], Wr=['wsTf'])
                        op('pool', lambda g=g: nc.gpsimd.affine_select(out=wblkf[:, g, :], in_=wblkf[:, g, :],
                                                                       compare_op=ALU.is_ge, fill=0.0, base=0,
                                                                       pattern=[[1, 32]], channel_multiplier=-1),
                           R=['wblkf'], Wr=['wblkf'] + [('wblkf', bi) for bi in range(NBH)])
                    op('dve', lambda: nc.vector.tensor_copy(out=wsT[:], in_=wsTf[:]), R=['wsTf'], Wr=['wsT'])
                    op('dve', lambda: nc.vector.tensor_copy(out=wblk[:], in_=wblkf[:]), R=['wblkf'], Wr=['wblk'])

                    pb, pk = bank()
                    for j in range(4):
                        op('pe', lambda j=j: nc.tensor.transpose(out=pb[:, j * 128:j * 128 + 2 * NBH],
                                                                 in_=tailsT[0:2 * NBH, j * 128:(j + 1) * 128],
                                                                 identity=ident_f[0:2 * NBH, 0:2 * NBH]),
                           R=[('st_stage', 0), 'tailsT', 'ident_f'], Wr=[pk])
                    op('act', lambda: nc.scalar.copy(
                        out=p_s[:, :, :, 0:2],
                        in_=pb[:, :].rearrange("p (j r) -> p j r", j=4)[:, :, 0:2 * NBH].rearrange("p j (b k) -> p j b k", k=2)),
                       R=[pk], Wr=['p_s'])
                    for q in range(2):
                        pb, pk = bank()
                        nr = nb2 * 30
                        for j in range(4):
                            op('pe', lambda j=j: nc.tensor.transpose(out=pb[:, j * 128:j * 128 + nr],
                                                                     in_=st_stage[0:nr, q, j * 128:(j + 1) * 128],
                                                                     identity=ident_f[0:nr, 0:nr]),
                               R=[('st_stage', 1 + q), 'ident_f'], Wr=[pk])
                        op('act', lambda: nc.scalar.copy(
                            out=glu_s[:, :, q * nb2:(q + 1) * nb2, 0:30],
                            in_=pb[:, :].rearrange("p (j r) -> p j r", j=4)[:, :, 0:nr].rearrange("p j (b k) -> p j b k", k=30)),
                           R=[pk], Wr=['glu_s'])


                if h == 0 and l == 0:
                    layer_prep_dma(0, 0)
                    layer_prep_compute(0, 0)
                nxt = (h, l + 1) if l + 1 < DEPTH else ((h + 1, 0) if h == 0 else None)

                self.ck(2)
                cv.reset(SCR0)
                p_pr = cv.get([128, 2, 2 + PH], F32)
                xa_sb = cv.get([128, 2, 512], F32)
                acc_a = cv.get([128, 2, 512], F32)
                u_sb = cv.get([128, 4, T], BF16)
                vg = cv.get([128, 4, 512], F32)
                vbf = cv.get([128, 4, 512], BF16)
                bst = cv.get([128, 4, 8], F32)

                i_xa = [0]

                def a_init(j):
                    sl = j % 2
                    if h == 0:
                        op('dve', lambda: nc.vector.memset(p_pr[:, sl, 0:2], 0.0), R=['wodone'], Wr=[('p_pr', sl)])
                    else:
                        op('dve', lambda: nc.vector.tensor_copy(out=p_pr[:, sl, 0:2], in_=pa_hist[:, l, j, :]),
                           R=[('pa_hist', l, j), 'wodone'], Wr=[('p_pr', sl)])

                def a_body(j, wt, wk, ti, s0, n):
                    sl = j % 2
                    smp = (ti == len(tts) - 1)
                    bx, kx = bank()
                    bg, kg_ = bank()
                    bc, kc_ = bank()
                    for seg, (bb, kk) in enumerate(((bx, kx), (bg, kg_), (bc, kc_))):
                        mm(bb[:, 0:n], [(wt[:, kc, seg, 0:128], xbf[:, kc, s0:s0 + n]) for kc in range(8)],
                           R=[wk] + allX(ti), Wr=[kk])
                    xs = i_xa[0] % 2
                    i_xa[0] += 1
                    op('act', lambda: nc.scalar.copy(out=xa_sb[:, xs, 0:n], in_=bx[:, 0:n]), R=[kx, 'wodone'], Wr=[('xa_sb', xs)])
                    wcol = lambda k: pT[:, l, WA + k * 4 + j:WA + k * 4 + j + 1]
                    if not smp:
                        pdst = p_pr[:, sl, 2 + s0:2 + s0 + n]
                        op('dve', lambda: nc.vector.tensor_tensor(out=pdst, in0=bc[:, 0:n], in1=xa_sb[:, xs, 0:n], op=ALU.mult),
                           R=[kc_, ('xa_sb', xs), 'wodone'], Wr=[('p_pr', sl)])
                        srcs = [p_pr[:, sl, s0 + k:s0 + k + n] for k in range(3)]
                        accv = acc_a[:, xs, 0:n]
                        gav = bg[:, 0:n]
                        yav = ya[:, j, s0:s0 + n]
                        pkey = ('p_pr', sl)
                    else:
                        pdst = p_s[:, j, :, 2:6]
                        op('dve', lambda: nc.vector.tensor_tensor(
                            out=pdst, in0=bc[:, 0:n].rearrange("p (b k) -> p b k", k=4),
                            in1=xa_sb[:, xs, 0:n].rearrange("p (b k) -> p b k", k=4), op=ALU.mult),
                           R=[kc_, ('xa_sb', xs)], Wr=['p_s'])
                        srcs = [p_s[:, j, :, k:k + 4] for k in range(3)]
                        accv = acc_a[:, xs, 0:n].rearrange("p (b k) -> p b k", k=4)
                        gav = bg[:, 0:n].rearrange("p (b k) -> p b k", k=4)
                        yav = ya[:, j, s0:s0 + n].rearrange("p (b k) -> p b k", k=4)
                        pkey = 'p_s'
                    op('dve', lambda: nc.vector.tensor_scalar(out=accv, in0=srcs[0], scalar1=wcol(0), scalar2=None, op0=ALU.mult),
                       R=[pkey, ('pT', l), 'wodone'], Wr=[('acc_a', xs)])
                    for k in (1, 2):
                        op('dve', lambda k=k: nc.vector.scalar_tensor_tensor(out=accv, in0=srcs[k], scalar=wcol(k), in1=accv,
                                                                             op0=ALU.mult, op1=ALU.add),
                           R=[pkey, ('pT', l), ('acc_a', xs)], Wr=[('acc_a', xs)])
                    op('dve', lambda: nc.vector.tensor_tensor(out=yav, in0=gav, in1=accv, op=ALU.mult),
                       R=[kg_, ('acc_a', xs), 'adone'], Wr=[('ya', j, ti)])

                def a_tail(j):
                    sl = j % 2
                    if h == 0:
                        op('act', lambda: nc.scalar.copy(out=pa_hist[:, l, j, :], in_=p_pr[:, sl, PH:PH + 2]),
                           R=[('p_pr', sl)], Wr=[('pa_hist', l, j)])
                    else:
                        op('act', lambda: nc.scalar.copy(out=tails[:, j, TA_P:TA_P + 2], in_=p_pr[:, sl, PH:PH + 2]),
                           R=[('p_pr', sl)], Wr=['tails'])
                    op('act', lambda: nc.scalar.copy(out=tails[:, j, TA_S:TA_S + 2 * NBH].rearrange("p (b k) -> p b k", k=2),
                                                     in_=p_s[:, j, :, 4:6]),
                       R=['p_s'], Wr=['tails'])


                for jp in range(2):
                    if jp == 0:
                        wtA, wkA = wnext(f"A{2 * jp}")
                        wtB, wkB = wnext(f"A{2 * jp + 1}", hold=1)
                        a_init(2 * jp)
                        a_init(2 * jp + 1)
                        for ti, (s0, n) in enumerate(tts):
                            a_body(2 * jp, wtA, wkA, ti, s0, n)
                            a_body(2 * jp + 1, wtB, wkB, ti, s0, n)
                        a_tail(2 * jp)
                        a_tail(2 * jp + 1)
                    else:
                        for j_ in (2 * jp, 2 * jp + 1):
                            wtA, wkA = wnext(f"A{j_}")
                            a_init(j_)
                            for ti, (s0, n) in enumerate(tts):
                                a_body(j_, wtA, wkA, ti, s0, n)
                            a_tail(j_)

                self.ck(3)
                if snap is not None:
                    self.apply_snapshot(snap)
                wt, wk = wnext("U")
                for j in range(4):
                    for ti, (s0, n) in enumerate(tts):
                        bb, kk = bank()
                        mm(bb[:, 0:n], [(wt[:, kc, j * 128:(j + 1) * 128], xbf[:, kc, s0:s0 + n]) for kc in range(8)],
                           R=[wk] + allX(ti), Wr=[kk])
                        op('act', lambda: nc.scalar.activation(out=u_sb[:, j, s0:s0 + n], in_=bb[:, 0:n], func=AF.Gelu),
                           R=[kk], Wr=[('u', j, ti)])
                wt, wk = wnext("V")
                NSB = 4

                def bstage1(m):
                    smp = (m == NM)
                    rows = S if smp else 128
                    c0 = PH if smp else m * 128
                    ti = tt_of_tile(m)
                    sl = m % NSB
                    bv, kv = bank()
                    mm(bv[0:rows, 0:512], [(xbf[:, kc, c0:c0 + rows], wt[:, kc, 0:512]) for kc in range(8)],
                       R=[wk] + allX(ti), Wr=[kv])
                    op('act', lambda: nc.scalar.activation(out=vg[0:rows, sl, :], in_=bv[0:rows, 0:512], func=AF.Gelu),
                       R=[kv], Wr=[('vg', sl)])
                    op('dve', lambda: nc.vector.bn_stats(out=bst[0:rows, sl, 0:6], in_=vg[0:rows, sl, :]),
                       R=[('vg', sl)], Wr=[('bst', sl)])
                    op('dve', lambda: nc.vector.bn_aggr(out=bst[0:rows, sl, 6:8], in_=bst[0:rows, sl, 0:6]),
                       R=[('bst', sl)], Wr=[('bst', sl)])
                    op('dve', lambda: nc.vector.tensor_scalar(out=bst[0:rows, sl, 7:8], in0=bst[0:rows, sl, 7:8],
                                                              scalar1=EPS, scalar2=None, op0=ALU.add),
                       R=[('bst', sl)], Wr=[('bst', sl)])
                    op('pool', lambda: nc.gpsimd.tensor_tensor(out=bst[0:rows, sl, 7:8], in0=bst[0:rows, sl, 7:8],
                                                               in1=neghalf[0:rows, 0:1], op=ALU.pow),
                       R=[('bst', sl), 'neghalf'], Wr=[('bst', sl)])
                    op('dve', lambda: nc.vector.tensor_scalar(out=vg[0:rows, sl, :], in0=vg[0:rows, sl, :],
                                                              scalar1=bst[0:rows, sl, 6:7], scalar2=bst[0:rows, sl, 7:8],
                                                              op0=ALU.subtract, op1=ALU.mult),
                       R=[('bst', sl), ('vg', sl)], Wr=[('vg', sl)])
                    op('dve', lambda: nc.vector.tensor_tensor(out=vg[0:rows, sl, :], in0=vg[0:rows, sl, :], in1=lnv_g[0:rows, :], op=ALU.mult),
                       R=[('vg', sl), 'lnv_g'], Wr=[('vg', sl)])
                    if smp:
                        op('dve', lambda: nc.vector.tensor_tensor(out=vg[0:rows, sl, :], in0=vg[0:rows, sl, :], in1=lnv_b[0:rows, :], op=ALU.add),
                           R=[('vg', sl), 'lnv_b'], Wr=[('vg', sl)])
                        dma('sp', o_v_s[l, 4 * b0:4 * b0 + S, :], vg[0:rows, sl, :], dsem_vo, R=[('vg', sl)])
                        op('dve', lambda: nc.vector.tensor_copy(out=vbf[0:rows, sl, :], in_=vg[0:rows, sl, :]),
                           R=[('vg', sl)], Wr=[('vbf', sl)])
                    else:
                        op('dve', lambda: nc.vector.tensor_tensor(out=vbf[0:rows, sl, :], in0=vg[0:rows, sl, :], in1=lnv_b[0:rows, :], op=ALU.add),
                           R=[('vg', sl), 'lnv_b'], Wr=[('vbf', sl)])

                def bstage2(m):
                    smp = (m == NM)
                    rows = S if smp else 128
                    c0 = PH if smp else m * 128
                    ti = tt_of_tile(m)
                    sl = m % NSB
                    bs_, ks_ = bank()
                    ncol = rows
                    for g in range(4):
                        if smp:
                            prs = [(vbf[0:rows, sl, g * 128:(g + 1) * 128], wblk[0:rows, g, :]),
                                   (ones_row[0:1, :], bsS_row[0:1, g, :, :].rearrange("p b k -> p (b k)"))]
                        else:
                            prs = [(vbf[0:rows, sl, g * 128:(g + 1) * 128], wsT[:, g, :]),
                                   (ones_row[0:1, :], bs_row[0:1, g, :])]
                        mm(bs_[:, g * ncol:(g + 1) * ncol], prs,
                           R=[('vbf', sl), 'wsT', 'wblk', 'bs_row', 'bsS_row', 'ones_row'], Wr=[ks_])
                    op('dve', lambda: nc.vector.tensor_tensor(out=yb[:, :, c0:c0 + ncol],
                                                              in0=bs_[:, 0:4 * ncol].rearrange("p (g t) -> p g t", g=4),
                                                              in1=u_sb[:, :, c0:c0 + ncol], op=ALU.mult),
                       R=[ks_] + [('u', j, ti) for j in range(4)], Wr=[('yb', m)])

                for m in range(NM + 1 + 2):
                    if m < NM + 1:
                        bstage1(m)
                    if m >= 2:
                        bstage2(m - 2)

                self.ck(4)
                self.barrier()
                cv.reset(SCR0)
                kT = cv.get([128, 4, NMEM], BF16)
                vP = cv.get([128, 2, W], BF16)
                glu = cv.get([128, 2, 30 + PH], BF16)
                glu_sb = cv.get([128, NBH, 34], BF16)
                dg = cv.get([128, 31, 128], BF16)
                cc = cv.get([128, 4, T], F32)
                sqt = cv.get([128, 2, 512], BF16)
                lnt = cv.get([128, 3, T], F32)
                t1b = cv.get([128, 2, 512], F32)
                sig = t1b

                i_x = 0
                tP = len(tts) - 2
                for j in range(4):
                    if j % 2 == 0:
                        wt, wk = wnext(f"C{j // 2}")
                    jj = j % 2
                    sl = j % 2
                    wc = lambda k: pT[:, l, WC + k * 4 + j:WC + k * 4 + j + 1]
                    bcol = pT[:, l, BC + j:BC + j + 1]
                    for k in range(31):
                        op('dve', lambda k=k: nc.vector.tensor_scalar(out=dg[:, k, :], in0=ident_b[:], scalar1=wc(k), scalar2=None, op0=ALU.mult),
                           R=['ident_b', ('pT', l)], Wr=[('dg', k)])
                    if h == 0:
                        op('dve', lambda: nc.vector.memset(glu[:, sl, 0:30], 0.0), Wr=[('glu', sl)])
                    else:
                        op('dve', lambda: nc.vector.tensor_copy(out=glu[:, sl, 0:30], in_=glu_hist[:, l, j, :]),
                           R=[('glu_hist', l, j)], Wr=[('glu', sl)])
                    for ti, (s0, n) in enumerate(tts):
                        smp = (ti == len(tts) - 1)
                        ba, ka = bank()
                        bb, kb = bank()
                        mm(ba[:, 0:n], [(wt[:, kc, 0, jj * 128:(jj + 1) * 128], xbf[:, kc, s0:s0 + n]) for kc in range(8)],
                           R=[wk] + allX(ti), Wr=[ka])
                        mm(bb[:, 0:n], [(wt[:, kc, 1, jj * 128:(jj + 1) * 128], xbf[:, kc, s0:s0 + n]) for kc in range(8)],
                           R=[wk] + allX(ti), Wr=[kb])
                        xs = i_x % 2
                        i_x += 1
                        op('act', lambda: nc.scalar.activation(out=sig[:, xs, 0:n], in_=bb[:, 0:n], func=AF.Sigmoid),
                           R=[kb], Wr=[('t1b', xs)])
                        if not smp:
                            op('dve', lambda: nc.vector.tensor_tensor(out=glu[:, sl, 30 + s0:30 + s0 + n], in0=ba[:, 0:n],
                                                                      in1=sig[:, xs, 0:n], op=ALU.mult),
                               R=[ka, ('t1b', xs)], Wr=[('glu', sl)])
                            if ti == tP:
                                tdst = glu_hist[:, l, j, :] if h == 0 else tails[:, j, TC_P:TC_P + 30]
                                tkey = ('glu_hist', l, j) if h == 0 else 'tails'
                                op('dve', lambda: nc.vector.tensor_tensor(out=tdst, in0=ba[:, n - 30:n], in1=sig[:, xs, n - 30:n], op=ALU.mult),
                                   R=[ka, ('t1b', xs)], Wr=[tkey])
                        else:
                            op('dve', lambda: nc.vector.tensor_tensor(out=glu_s[:, j, :, 30:34],
                                                                      in0=ba[:, 0:n].rearrange("p (b k) -> p b k", k=4),
                                                                      in1=sig[:, xs, 0:n].rearrange("p (b k) -> p b k", k=4), op=ALU.mult),
                               R=[ka, ('t1b', xs)], Wr=['glu_s'])
                            op('act', lambda: nc.scalar.copy(out=glu_sb[:, :, :], in_=glu_s[:, j, :, :]), R=['glu_s'], Wr=['glu_sb'])
                            op('act', lambda: nc.scalar.copy(out=tails[:, j, TC_S:TC_S + 4 * NBH].rearrange("p (b k) -> p b k", k=4),
                                                             in_=glu_s[:, j, :, 30:34]),
                               R=['glu_s'], Wr=['tails'])
                    dgk = [('dg', k) for k in range(31)]
                    for ti, (s0, n) in enumerate(tts):
                        smp = (ti == len(tts) - 1)
                        bk_, kk_ = bank()
                        if not smp:
                            mm(bk_[:, 0:n], [(dg[:, k, :], glu[:, sl, s0 + k:s0 + k + n]) for k in range(31)],
                               R=dgk + [('glu', sl)], Wr=[kk_])
                        else:
                            mm(bk_[:, 0:n].rearrange("p (b k) -> p b k", k=4), [(dg[:, k, :], glu_sb[:, :, k:k + 4]) for k in range(31)],
                               R=dgk + ['glu_sb'], Wr=[kk_])
                        op('act', lambda: nc.scalar.activation(out=cc[:, j, s0:s0 + n], in_=bk_[:, 0:n], func=AF.Identity, bias=bcol),
                           R=[kk_, ('pT', l)], Wr=[('cc', j, ti)])

                pb, pk = bank()
                for j in range(4):
                    op('pe', lambda j=j: nc.tensor.transpose(out=pb[0:NT, j * 128:(j + 1) * 128], in_=tails[:, j, :], identity=ident_f[:]),
                       R=['tails', 'ident_f'], Wr=[pk])
                op('act', lambda: nc.scalar.copy(out=tailsT[0:NT, :], in_=pb[0:NT, :]), R=[pk], Wr=['tailsT'])
                if h == 1:
                    dma('sp', o_ca_p[l], tailsT[TA_P:TA_P + 2, :], dsem_tail, R=['tailsT'])
                    dma('sp', o_cc_p[l], tailsT[TC_P:TC_P + 30, :], dsem_tail, R=['tailsT'])
                dma('sp', o_ca_s[l, 2 * b0:2 * b0 + 2 * NBH, :], tailsT[TA_S:TA_S + 2 * NBH, :], dsem_tail, R=['tailsT'])
                for bi in range(NBH):
                    dma('sp', o_cc_s[l, b0 + bi, 26:30, :], tailsT[TC_S + 4 * bi:TC_S + 4 * bi + 4, :], dsem_tail, R=['tailsT'])

                def cck(j, ti):
                    return ('cc', j, ti)

                def ln_acc(srcf, keyf, nch, ones_m, n):
                    bm, km = bank(pin=True)
                    be, ke = bank(pin=True)
                    for c in range(nch):
                        xs = c % 2
                        op('act', lambda c=c, xs=xs: nc.scalar.activation(out=sqt[:, xs, 0:n], in_=srcf(c), func=AF.Square),
                           R=[keyf(c)], Wr=[('sqt', xs)])
                        op('pe', lambda c=c: nc.tensor.matmul(bm[:, 0:n], ones_m[:], srcf(c), start=(c == 0), stop=(c == nch - 1)),
                           R=[keyf(c), 'onesW', 'onesD'], Wr=[km])
                        op('pe', lambda c=c, xs=xs: nc.tensor.matmul(be[:, 0:n], (onesWb if ones_m is onesW else onesDb)[:], sqt[:, xs, 0:n],
                                                                     start=(c == 0), stop=(c == nch - 1)),
                           R=[('sqt', xs), 'onesWb', 'onesDb'], Wr=[ke])
                    return (bm, km, be, ke)

                def ln_fin(acc, lo, n, kt):
                    bm, km, be, ke = acc
                    L0, L1, L2 = lnt[:, 0, lo:lo + n], lnt[:, 1, lo:lo + n], lnt[:, 2, lo:lo + n]
                    op('act', lambda: nc.scalar.activation(out=L0, in_=bm[:, 0:n], func=AF.Square), R=[km], Wr=[('lnt', 0, kt)])
                    op('dve', lambda: nc.vector.scalar_tensor_tensor(out=L0, in0=be[:, 0:n], scalar=EPS, in1=L0,
                                                                     op0=ALU.add, op1=ALU.subtract),
                       R=[ke, ('lnt', 0, kt)], Wr=[('lnt', 0, kt)])
                    op('act', lambda: nc.scalar.activation(out=L0, in_=L0, func=AF.Ln),
                       R=[('lnt', 0, kt)], Wr=[('lnt', 0, kt)])
                    op('act', lambda: nc.scalar.activation(out=L1, in_=L0, func=AF.Exp, scale=-0.5),
                       R=[('lnt', 0, kt)], Wr=[('lnt', 1, kt)])
                    op('dve', lambda: nc.vector.scalar_tensor_tensor(out=L2, in0=bm[:, 0:n], scalar=-1.0, in1=L1,
                                                                     op0=ALU.mult, op1=ALU.mult),
                       R=[km, ('lnt', 1, kt)], Wr=[('lnt', 2, kt)])
                    self.unpin(km)
                    self.unpin(ke)

                def ln_apply(buf, nch, keys, s0, n, lo, kt):
                    for c0 in range(0, nch, 4):
                        c1 = min(nch, c0 + 4)
                        gk = keys[c0:c1]
                        v = buf[:, c0:c1, s0:s0 + n]
                        rb = lnt[:, 1, lo:lo + n].unsqueeze(1).broadcast_to([128, c1 - c0, n])
                        op('dve', lambda: nc.vector.tensor_tensor(out=v, in0=v, in1=rb, op=ALU.mult),
                           R=[('lnt', 1, kt)] + gk, Wr=gk)
                        npool = 0
                        nd = (c1 - c0) - npool
                        v1 = buf[:, c0:c0 + nd, s0:s0 + n]
                        nb1 = lnt[:, 2, lo:lo + n].unsqueeze(1).broadcast_to([128, nd, n])
                        op('dve', lambda: nc.vector.tensor_tensor(out=v1, in0=v1, in1=nb1, op=ALU.add),
                           R=[('lnt', 2, kt)] + gk[:nd], Wr=gk[:nd])
                        if npool:
                            v2 = buf[:, c0 + nd:c1, s0:s0 + n]
                            nb2_ = lnt[:, 2, lo:lo + n].unsqueeze(1).broadcast_to([128, npool, n])
                            op('pool', lambda: nc.gpsimd.tensor_tensor(out=v2, in0=v2, in1=nb2_, op=ALU.add),
                               R=[('lnt', 2, kt)] + gk[nd:], Wr=gk[nd:])

                def c_acc(ti):
                    s0, n = tts[ti]
                    return ln_acc(lambda c: cc[:, c, s0:s0 + n], lambda c: cck(c, ti), 4, onesW, n)

                def c_app(ti):
                    s0, n = tts[ti]
                    ln_apply(cc, 4, [cck(j, ti) for j in range(4)], s0, n, s0, ti)
                    for j in range(4):
                        op('act', lambda j=j: nc.scalar.activation(out=yc[:, j, s0:s0 + n], in_=cc[:, j, s0:s0 + n], func=AF.Silu,
                                                                   bias=pcol(l, LCB + j), scale=pcol(l, LCG + j)),
                           R=[cck(j, ti), ('pT', l)], Wr=[('yc', j, ti)])
                def kv_proj():
                    kvout = t1b
                    wkk, kkk = wnext("KVK")
                    for h4 in range(4):
                        bb, kk = bank()
                        mm(bb[:, 0:NMEM], [(wkk[:, kc, h4 * 128:(h4 + 1) * 128], memT[:, kc, :]) for kc in range(8)],
                           R=[kkk, 'memT'], Wr=[kk])
                        op('act', lambda: nc.scalar.copy(out=kT[:, h4, :], in_=bb[:, 0:NMEM]), R=[kk], Wr=['kT'])
                    i_o = 0
                    for isv in (False, True):
                        if isv:
                            wv_, kv_ = wnext("KVV")
                            dst = o_mv
                        else:
                            wv_, kv_, dst = wkk, kkk, o_mk
                        for mc in range(2):
                            bb, kk = bank()
                            mm(bb[:, 0:W], [(memT[:, kc, mc * 128:(mc + 1) * 128], wv_[:, kc, :]) for kc in range(8)],
                               R=[kv_, 'memT'], Wr=[kk])
                            sl = i_o % 2
                            i_o += 1
                            op('dve', lambda: nc.vector.tensor_copy(out=kvout[:, sl, :], in_=bb[:, 0:W]), R=[kk], Wr=[('t1b', sl)])
                            if isv:
                                op('act', lambda: nc.scalar.copy(out=vP[:, mc, :], in_=bb[:, 0:W]), R=[kk], Wr=['vP'])
                            dma('sp', dst[l, mc * 128:(mc + 1) * 128, :], kvout[:, sl, :], dsem_kvo[sl], R=[('t1b', sl)])

                nt_ = len(tts)
                caccs = {}
                for step in range(nt_ + 2):
                    if step == nt_:
                        kv_proj()
                    if step < nt_:
                        caccs[step] = c_acc(step)
                    if 0 <= step - 1 < nt_:
                        ln_fin(caccs[step - 1], tts[step - 1][0], tts[step - 1][1], step - 1)
                    if 0 <= step - 2 < nt_:
                        c_app(step - 2)

                self.ck(5)
                self.barrier()
                cv.reset(SCR0)
                kT = cv.get([128, 4, NMEM], BF16)
                vP = cv.get([128, 2, W], BF16)
                qT = cv.get([128, 4, T], BF16)
                e_bf = cv.get([128, 2, 2, 512], BF16)
                rden = cv.get([128, 2, 512], F32)
                NKS = 4
                kS_raw = cv.get([128, NKS, 2, W], BF16)
                kTs = cv.get([128, NKS, 4, NMEM], BF16)
                vS = cv.get([128, 2, 2, W], BF16)
                e_s = cv.get([128, 8 * S], BF16)
                rden_s = cv.get([128, 4 * S], F32)

                def issue_ks(bi):
                    sl = bi % NKS
                    dma('pool', kS_raw[:, sl, :, :], cache_k[l, b0 + bi].rearrange("(mc p) c -> p mc c", p=128), dsem_ks[sl],
                        Wr=[('kS_raw', sl)])

                def vloc(bi):
                    if NBH == 8 and 2 <= bi <= 5:
                        sl = bi - 2
                        return kS_raw[:, sl, :, :], ('kS_raw', sl), dsem_ks[sl]
                    sl = bi % 2
                    return vS[:, sl, :, :], ('vS', sl), dsem_vs[sl]

                def issue_vs(bi):
                    buf, key, ds = vloc(bi)
                    dma('pool', buf, cache_v[l, b0 + bi].rearrange("(mc p) c -> p mc c", p=128), ds, Wr=[key])
                for bi_ in range(min(NKS, NBH)):
                    issue_ks(bi_)
                issue_vs(0)
                issue_vs(1)
                wq, kq = wnext("Q")
                for h4 in range(4):
                    for ti, (s0, n) in enumerate(tts):
                        bb, kk = bank()
                        mm(bb[:, 0:n], [(wq[:, kc, h4 * 128:(h4 + 1) * 128], xbf[:, kc, s0:s0 + n]) for kc in range(8)],
                           R=[kq] + allX(ti), Wr=[kk])
                        op('act', lambda: nc.scalar.copy(out=qT[:, h4, s0:s0 + n], in_=bb[:, 0:n]), R=[kk], Wr=[('qT', h4, ti)])
                i_x = 0
                for ti, (s0, n) in enumerate(tts[:-1]):
                    for h4 in range(4):
                        xs = i_x % 2
                        i_x += 1
                        bsc = []
                        for mc in range(2):
                            bb, kk = bank()
                            mm(bb[:, 0:n], [(kT[:, h4, mc * 128:(mc + 1) * 128], qT[:, h4, s0:s0 + n])], R=['kT', ('qT', h4, ti)], Wr=[kk])
                            bsc.append((bb, kk))
                        for mc in range(2):
                            bb, kk = bsc[mc]
                            op('act', lambda mc=mc, bb=bb: nc.scalar.activation(out=e_bf[:, xs, mc, 0:n], in_=bb[:, 0:n], func=AF.Exp, scale=QSCALE),
                               R=[kk], Wr=[('e_bf', xs, mc)])
                        bd, kd = bank()
                        mm(bd[:, 0:n], [(ones_b[:], e_bf[:, xs, mc, 0:n]) for mc in range(2)], R=[('e_bf', xs, 0), ('e_bf', xs, 1), 'ones_b'], Wr=[kd])
                        bp, kp = bank()
                        mm(bp[:, 0:n], [(vP[:, mc, h4 * 128:(h4 + 1) * 128], e_bf[:, xs, mc, 0:n]) for mc in range(2)],
                           R=[('e_bf', xs, 0), ('e_bf', xs, 1), 'vP'], Wr=[kp])
                        op('act', lambda: nc.scalar.activation(out=rden[:, xs, 0:n], in_=bd[:, 0:n], func=AF.Ln), R=[kd], Wr=[('rden', xs)])
                        op('act', lambda: nc.scalar.activation(out=rden[:, xs, 0:n], in_=rden[:, xs, 0:n], func=AF.Exp, scale=-1.0),
                           R=[('rden', xs)], Wr=[('rden', xs)])
                        op('dve', lambda: nc.vector.tensor_tensor(out=yx[:, h4, s0:s0 + n], in0=bp[:, 0:n], in1=rden[:, xs, 0:n], op=ALU.mult),
                           R=[kp, ('rden', xs)], Wr=[('yx', h4, ti)])
                tS = len(tts) - 1
                bsS, ksS = bank(pin=True)

                def ks_stage1(bi):
                    sl = bi % NKS
                    ptb, ptk = bank()
                    ptv = ptb[:, :].bitcast(BF16)
                    for h4 in range(4):
                        for mc in range(2):
                            o0 = (h4 * 2 + mc) * 128
                            op('pe', lambda h4=h4, mc=mc, o0=o0: nc.tensor.transpose(out=ptv[:, o0:o0 + 128],
                                                                                     in_=kS_raw[:, sl, mc, h4 * 128:(h4 + 1) * 128],
                                                                                     identity=ident_b[:]),
                               R=[('kS_raw', sl), 'ident_b'], Wr=[ptk])
                    op('dve', lambda: nc.vector.tensor_copy(out=kTs[:, sl, :, :].rearrange("p a b -> p (a b)"), in_=ptv[:, :]),
                       R=[ptk], Wr=[('kTs', sl)])
                    if bi + NKS < NBH:
                        issue_ks(bi + NKS)
                    elif NBH == 8:
                        issue_vs(bi - 2)

                def ks_stage2(bi):
                    sl = bi % NKS
                    for h4 in range(4):
                        for mc in range(2):
                            o0 = ((h4 * 2 + mc) * NBH + bi) * 4
                            mm(bsS[:, o0:o0 + 4], [(kTs[:, sl, h4, mc * 128:(mc + 1) * 128], qT[:, h4, PH + 4 * bi:PH + 4 * bi + 4])],
                               R=[('kTs', sl), ('qT', h4, tS)], Wr=[ksS])
                for step in range(NBH + 2):
                    if step < NBH:
                        ks_stage1(step)
                    if step >= 2:
                        ks_stage2(step - 2)
                op('act', lambda: nc.scalar.activation(out=e_s[:, :], in_=bsS[:, 0:8 * S], func=AF.Exp, scale=QSCALE), R=[ksS], Wr=['e_s'])
                self.unpin(ksS)
                bdS, kdS = bank()
                e_s4 = e_s[:, :].rearrange("p (h m s) -> p h m s", h=4, m=2)
                mm(bdS[:, 0:4 * S].rearrange("p (h s) -> p h s", h=4), [(ones_b[:], e_s4[:, :, mc, :]) for mc in range(2)],
                   R=['e_s', 'ones_b'], Wr=[kdS])
                bpS, kpS = bank()
                for bi in range(NBH):
                    vbuf, vkey, _ = vloc(bi)
                    for h4 in range(4):
                        o0 = (h4 * NBH + bi) * 4
                        mm(bpS[:, o0:o0 + 4],
                           [(vbuf[:, mc, h4 * 128:(h4 + 1) * 128], e_s[:, ((h4 * 2 + mc) * NBH + bi) * 4:((h4 * 2 + mc) * NBH + bi) * 4 + 4])
                            for mc in range(2)],
                           R=[vkey, 'e_s'], Wr=[kpS])
                    if NBH == 8:
                        if bi < 2:
                            issue_vs(bi + 6)
                    elif bi + 2 < NBH:
                        issue_vs(bi + 2)
                op('act', lambda: nc.scalar.activation(out=rden_s[:, :], in_=bdS[:, 0:4 * S], func=AF.Ln), R=[kdS], Wr=['rden_s'])
                op('act', lambda: nc.scalar.activation(out=rden_s[:, :], in_=rden_s[:, :], func=AF.Exp, scale=-1.0), R=['rden_s'], Wr=['rden_s'])
                op('dve', lambda: nc.vector.tensor_tensor(out=yx[:, :, PH:T], in0=bpS[:, 0:4 * S].rearrange("p (h s) -> p h s", h=4),
                                                          in1=rden_s[:, :].rearrange("p (h s) -> p h s", h=4), op=ALU.mult),
                   R=[kpS, 'rden_s'], Wr=[('yx', h4, tS) for h4 in range(4)])

                self.ck(6)
                self.barrier()
                cv.reset(SCRG)
                gsig = cv.get([128, 2, 512], F32)
                gacc = cv.get([128, 2, 512], F32)
                gtmp = cv.get([128, 2, 512], F32)
                sqt = cv.get([128, 2, 512], BF16)
                lnt = cv.get([128, 3, T], F32)

                def ykeys(nb, ti):
                    nm = ('ya', 'yb', 'yc', 'yx')[nb]
                    if nb == 1:
                        if ti == len(tts) - 1:
                            return [('yb', NM)]
                        s0, n = tts[ti]
                        return [('yb', m) for m in range(s0 // 128, (s0 + n) // 128)]
                    return [(nm, j, ti) for j in range(4)]

                i_x = 0
                i_a = 0
                for dc in range(8):
                    wg, kg_ = wnext(f"G{dc}")
                    wb, kb_ = wnext(f"BR{dc}", hold=1)
                    for dd in range(1):
                        for ti, (s0, n) in enumerate(tts):
                            asl = i_a % 2
                            i_a += 1
                            for nb in range(4):
                                bg, kbg = bank()
                                bb, kbb = bank()
                                mm(bg[:, 0:n], [(wg[:, kc, nb, dd * 128:(dd + 1) * 128], xbf[:, kc, s0:s0 + n]) for kc in range(8)],
                                   R=[kg_] + allX(ti), Wr=[kbg])
                                ytok = mm(bb[:, 0:n], [(wb[:, nb, wc_, dd * 128:(dd + 1) * 128], ys[nb][:, wc_, s0:s0 + n]) for wc_ in range(4)],
                                          R=[kb_] + ykeys(nb, ti), Wr=[kbb])
                                self.lw['ydone'] = ytok
                                xs = i_x % 2
                                i_x += 1
                                op('act', lambda: nc.scalar.activation(out=gsig[:, xs, 0:n], in_=bg[:, 0:n], func=AF.Sigmoid),
                                   R=[kbg], Wr=[('gsig', xs)])
                                if nb == 0:
                                    op('dve', lambda: nc.vector.tensor_tensor(out=gacc[:, asl, 0:n], in0=bb[:, 0:n], in1=gsig[:, xs, 0:n], op=ALU.mult),
                                       R=[kbb, ('gsig', xs)], Wr=[('gacc', asl)])
                                else:
                                    op('dve', lambda: nc.vector.tensor_tensor(out=gtmp[:, xs, 0:n], in0=bb[:, 0:n], in1=gsig[:, xs, 0:n], op=ALU.mult),
                                       R=[kbb, ('gsig', xs)], Wr=[('gtmp', xs)])
                                    if nb < 3:
                                        op('dve', lambda: nc.vector.tensor_tensor(out=gacc[:, asl, 0:n], in0=gacc[:, asl, 0:n], in1=gtmp[:, xs, 0:n], op=ALU.add),
                                           R=[('gacc', asl), ('gtmp', xs)], Wr=[('gacc', asl)])
                                    else:
                                        op('dve', lambda: nc.vector.tensor_tensor(out=mixin[:, dc, s0:s0 + n], in0=gacc[:, asl, 0:n], in1=gtmp[:, xs, 0:n], op=ALU.add),
                                           R=[('gacc', asl), ('gtmp', xs)], Wr=[('mixin', dc, ti)])

                def layernorm_all(gcol, bcol, agcol, abcol, res_plain):
                    def acc_(ti):
                        s0, n = tts[ti]
                        return ln_acc(lambda c: Rr[:, c, s0:s0 + n], lambda c: Rk(c, ti), 8, onesD, n)

                    def fin_(ti, a):
                        s0, n = tts[ti]
                        ln_fin(a, s0, n, ti)

                    def app_(ti):
                        s0, n = tts[ti]
                        ln_apply(Rr, 8, [Rk(c, ti) for c in range(8)], s0, n, s0, ti)
                        for c in range(8):
                            if res_plain:
                                op('act', lambda c=c: nc.scalar.activation(out=Rr[:, c, s0:s0 + n], in_=Rr[:, c, s0:s0 + n], func=AF.Identity,
                                                                           bias=pcol(l, bcol + c), scale=pcol(l, gcol + c)),
                                   R=[('pT', l)], Wr=[Rk(c, ti)])
                            else:
                                op('dve', lambda c=c: nc.vector.tensor_scalar(out=xbf[:, c, s0:s0 + n], in0=Rr[:, c, s0:s0 + n],
                                                                              scalar1=pcol(l, gcol + c), scalar2=pcol(l, bcol + c),
                                                                              op0=ALU.mult, op1=ALU.add),
                                   R=[Rk(c, ti), ('pT', l)], Wr=[Xk(c, ti)])
                                op('act', lambda c=c: nc.scalar.activation(out=Rr[:, c, s0:s0 + n], in_=Rr[:, c, s0:s0 + n], func=AF.Identity,
                                                                           bias=pcol(l, abcol + c), scale=pcol(l, agcol + c)),
                                   R=[('pT', l)], Wr=[Rk(c, ti)])
                    nt = len(tts)
                    accs = {}
                    for step in range(nt + 2):
                        if step < nt:
                            accs[step] = acc_(step)
                        if 0 <= step - 1 < nt:
                            fin_(step - 1, accs[step - 1])
                        if 0 <= step - 2 < nt:
                            app_(step - 2)

                for hf in range(2):
                    wo, kwo = wnext(f"WO{hf}")
                    for oo in range(4):
                        oc = hf * 4 + oo
                        for ti, (s0, n) in enumerate(tts):
                            bb, kk = bank()
                            self.lw['wodone'] = mm(bb[:, 0:n], [(wo[:, kc, oo * 128:(oo + 1) * 128], mixin[:, kc, s0:s0 + n]) for kc in range(8)],
                                                   R=[kwo] + [('mixin', kc, ti) for kc in range(8)], Wr=[kk])
                            op('dve', lambda: nc.vector.tensor_tensor(out=Rr[:, oc, s0:s0 + n], in0=Rr[:, oc, s0:s0 + n], in1=bb[:, 0:n], op=ALU.add),
                               R=[kk, Rk(oc, ti)], Wr=[Rk(oc, ti)])
                layernorm_all(L1G, L1B, AG1, AB1, False)

                self.ck(7)
                _save = cv.off
                cv.reset(0)
                a_sb = cv.get([128, 16, T], BF16)
                cv.reset(_save)
                a1 = gsig
                i_x = 0
                if nxt is not None:
                    layer_prep_dma(*nxt)
                for fh in range(2):
                    if fh == 1 and nxt is not None:
                        layer_prep_compute(*nxt)
                    for gp in range(2):
                        paired = (fh == 0 and gp == 0)
                        if paired:
                            wu0, ku0 = wnext(f"UP{fh * 4 + gp * 2}")
                            wu1, ku1 = wnext(f"UP{fh * 4 + gp * 2 + 1}", hold=1)
                            sched = [(ti, s0, n, g, wu, ku) for ti, (s0, n) in enumerate(tts)
                                     for (g, wu, ku) in ((gp * 2, wu0, ku0), (gp * 2 + 1, wu1, ku1))]
                        else:
                            sched = None
                        for gsel in ([None] if paired else [gp * 2, gp * 2 + 1]):
                            if not paired:
                                wu_, ku_ = wnext(f"UP{fh * 4 + gsel}")
                                sched = [(ti, s0, n, gsel, wu_, ku_) for ti, (s0, n) in enumerate(tts)]
                            for (ti, s0, n, g, wu, ku) in sched:
                              for _once in (0,):
                                for ff in range(4):
                                    fcl = g * 4 + ff
                                    fc = fh * 16 + fcl
                                    bb, kk = bank()
                                    mm(bb[:, 0:n], [(wu[:, kc, ff * 128:(ff + 1) * 128], xbf[:, kc, s0:s0 + n]) for kc in range(8)],
                                       R=[ku] + allX(ti), Wr=[kk])
                                    xs = i_x % 2
                                    i_x += 1
                                    op('act', lambda: nc.scalar.activation(out=a1[:, xs, 0:n], in_=bb[:, 0:n], func=AF.Relu, bias=pcol(l, BUP + fc)),
                                       R=[kk, ('pT', l)], Wr=[('gsig', xs)])
                                    op('dve', lambda: nc.vector.tensor_tensor(out=a_sb[:, fcl, s0:s0 + n], in0=a1[:, xs, 0:n], in1=a1[:, xs, 0:n], op=ALU.mult),
                                       R=[('gsig', xs), 'ydone'], Wr=[('a', fcl, ti)])
                    for dcp in range(4):
                        wd, kd_ = wnext(f"DN{fh}_{dcp}")
                        for dd in range(2):
                            dc = dcp * 2 + dd
                            for ti, (s0, n) in enumerate(tts):
                                bb, kk = bank()
                                self.lw['adone'] = mm(bb[:, 0:n], [(wd[:, fcl, dd * 128:(dd + 1) * 128], a_sb[:, fcl, s0:s0 + n]) for fcl in range(16)],
                                                      R=[kd_] + [('a', fcl, ti) for fcl in range(16)], Wr=[kk])
                                op('dve', lambda: nc.vector.tensor_tensor(out=Rr[:, dc, s0:s0 + n], in0=Rr[:, dc, s0:s0 + n], in1=bb[:, 0:n], op=ALU.add),
                                   R=[kk, Rk(dc, ti)], Wr=[Rk(dc, ti)])
                layernorm_all(L2G, L2B, AG2, AB2, last)

                if last:
                    self.barrier()
                    cv.reset(0)
                    yout = cv.get([128, 2, D], F32)
                    for m in range(NM + 1):
                        smp = (m == NM)
                        rows = S if smp else 128
                        c0 = PH if smp else m * 128
                        ti = tt_of_tile(m)
                        sl = m % 2
                        for kg in range(2):
                            pb, pk = bank()
                            for i in range(4):
                                kc = kg * 4 + i
                                op('pe', lambda i=i, kc=kc: nc.tensor.transpose(out=pb[0:rows, i * 128:(i + 1) * 128],
                                                                                in_=Rr[:, kc, c0:c0 + rows], identity=ident_f[:]),
                                   R=[Rk(kc, ti), 'ident_f'], Wr=[pk])
                            if kg == 0:
                                op('act', lambda: nc.scalar.copy(out=yout[0:rows, sl, 0:512], in_=pb[0:rows, :]), R=[pk], Wr=[('yout', sl, 0)])
                            else:
                                op('dve', lambda: nc.vector.tensor_copy(out=yout[0:rows, sl, 512:1024], in_=pb[0:rows, :]), R=[pk], Wr=[('yout', sl, 1)])
                        dst = y_sample[h * S:(h + 1) * S, :] if smp else y_prompt[h * PH + m * 128:h * PH + (m + 1) * 128, :]
                        dma('sp', dst, yout[0:rows, sl, :], dsem_y[sl], R=[('yout', sl, 0), ('yout', sl, 1)])


_CACHE = {}


def _get_nc(SEQ, NB, DEPTH):
    key = (SEQ, NB, DEPTH)
    if key not in _CACHE:
        _CACHE[key] = Builder(SEQ, NB, DEPTH).build()
    return _CACHE[key]


def make_in_map(inputs, c, SEQ, NB, DEPTH):
    f = lambda a: np.ascontiguousarray(np.asarray(a, dtype=np.float32))
    g = inputs
    m = {
        "x_prompt": f(g["x_prompt"][c]),
        "x_sample": f(g["x_sample"][c * NB:(c + 1) * NB]).reshape(NB * 4, D),
        "mem_prompt": f(g["mem_prompt"][c]),
        "state_conv_a": f(g["state_conv_a"][:, c * NB:(c + 1) * NB]).reshape(DEPTH, NB * 2, W),
        "state_conv_c": f(g["state_conv_c"][:, c * NB:(c + 1) * NB]),
        "cache_mem_k": f(g["cache_mem_k"][:, c * NB:(c + 1) * NB]).reshape(DEPTH, NB, NMEM, W),
        "cache_mem_v": f(g["cache_mem_v"][:, c * NB:(c + 1) * NB]).reshape(DEPTH, NB, NMEM, W),
        "w_in": f(g["w_in"]),
        "w_conv_a": f(g["w_conv_a"]).reshape(DEPTH, 12, 128),
        "ln_v_g": f(g["ln_v_g"]), "ln_v_b": f(g["ln_v_b"]),
        "w_s": f(g["w_s"]), "b_s": f(g["b_s"]),
        "w_conv_c": f(g["w_conv_c"]).reshape(DEPTH, 124, 128),
        "b_conv_c": f(g["b_conv_c"]).reshape(DEPTH, 4, 128),
        "ln_c_g": f(g["ln_c_g"]).reshape(DEPTH, 4, 128),
        "ln_c_b": f(g["ln_c_b"]).reshape(DEPTH, 4, 128),
        "w_mem_kv": f(g["w_mem_kv"]), "w_out_br": f(g["w_out_br"]), "w_o": f(g["w_o"]),
        "ln1_g": f(g["ln1_g"]).reshape(DEPTH, 8, 128), "ln1_b": f(g["ln1_b"]).reshape(DEPTH, 8, 128),
        "w_up": f(g["w_up"]), "b_up": f(g["b_up"]).reshape(DEPTH, 32, 128),
        "w_down": f(g["w_down"]), "b_down": f(g["b_down"]).reshape(DEPTH, 8, 128),
        "ln2_g": f(g["ln2_g"]).reshape(DEPTH, 8, 128), "ln2_b": f(g["ln2_b"]).reshape(DEPTH, 8, 128),
    }
    return m


def run(inputs, n_cores, SEQ, NB, DEPTH, trace=False):
    nc = _get_nc(SEQ, NB, DEPTH)
    in_maps = [make_in_map(inputs, c, SEQ, NB, DEPTH) for c in range(n_cores)]
    res = run_bass_kernel_spmd(nc, in_maps, core_ids=list(range(n_cores)), trace=trace)
    r = res.results
    B = n_cores
    yp = np.stack([r[c]["y_prompt"] for c in range(B)], 0)
    ysm = np.concatenate([r[c]["y_sample"].reshape(NB, 4, D) for c in range(B)], 0)
    cap = np.stack([r[c]["new_conv_a_prompt"] for c in range(B)], 1)
    ccp = np.stack([r[c]["new_conv_c_prompt"] for c in range(B)], 1)
    mk = np.stack([r[c]["mem_k_prompt"].reshape(DEPTH, NMEM, 4, 128) for c in range(B)], 1)
    mv = np.stack([r[c]["mem_v_prompt"].reshape(DEPTH, NMEM, 4, 128) for c in range(B)], 1)
    cas = np.concatenate([r[c]["new_conv_a_sample"].reshape(DEPTH, NB, 2, W) for c in range(B)], 1)
    ccs = np.concatenate([r[c]["new_conv_c_sample"] for c in range(B)], 1)
    vs = np.concatenate([r[c]["chunk_v_sample"].reshape(DEPTH, NB, 4, W) for c in range(B)], 1)
    outs = tuple(np.ascontiguousarray(o, dtype=np.float32) for o in (yp, ysm, cap, ccp, mk, mv, cas, ccs, vs))
    return outs, res


def kernel(**inputs):
    outs, _ = run(inputs, 8, 2048, 16, 4)
    return outs
```
